# Optimizing a Trainium2 kernel written in Bass

```python
import jax, jax.numpy as jnp
from jax import lax
import numpy as np

D_MODEL = 1024
BATCH = 2
SEQ = 8192
DEPTH = 1
DEC_BATCH = 128
DEC_SEQ = 4
PAST_LEN = 2048
PAGE_SIZE = 128

MIX_WIDTH = D_MODEL
MIX_A = MIX_WIDTH // 2
A_GROUPS = 8
A_DG = MIX_A // A_GROUPS
CHUNK = 128
MIX_B = MIX_WIDTH - MIX_A
HEAD_DIM = 64
N_HEADS = MIX_B // HEAD_DIM
N_KV = 4
Q_PER_KV = N_HEADS // N_KV
KV_W = N_KV * HEAD_DIM
L_CMP = 32
STRIDE = 16
CMP_HID = 256
L_SLC = 64
N_SEL = 16
WINDOW = 512
Q_BLOCK = 128
D_FF = -(-(8 * D_MODEL) // (3 * 256)) * 256
IN_COLS = 2 * MIX_A + MIX_B + 6 * KV_W + 3 * N_HEADS
ROPE_THETA = 10000.0
EPS = 1e-6
FORCED_SCORE = 1e4

kernel_name = 'hymba_gmlp_nsa_decoder_step'


def rmsnorm(x, g):
    xf = x.astype(jnp.float32)
    y = xf * lax.rsqrt(jnp.mean(xf * xf, axis=-1, keepdims=True) + EPS)
    return (y * g.astype(jnp.float32)).astype(x.dtype)


def modulate(x, g, shift, scale):
    return rmsnorm(x, g) * (1 + scale[:, None]) + shift[:, None]


def rope(x, pos):
    half = HEAD_DIM // 2
    inv = ROPE_THETA ** (-jnp.arange(half, dtype=jnp.float32) / half)
    ang = pos.astype(jnp.float32)[:, None] * inv[None, :]
    cos, sin = jnp.cos(ang)[:, None, :], jnp.sin(ang)[:, None, :]
    xf = x.astype(jnp.float32)
    x1, x2 = xf[..., :half], xf[..., half:]
    return jnp.concatenate([x1 * cos - x2 * sin, x2 * cos + x1 * sin], axis=-1).astype(x.dtype)


def masked_probs(s, mask):
    s = jnp.where(mask, s, -jnp.inf)
    m = jnp.max(s, axis=-1, keepdims=True)
    m = jnp.where(jnp.isfinite(m), m, 0.0)
    p = jnp.where(mask, jnp.exp(s - m), 0.0)
    return p / jnp.maximum(jnp.sum(p, axis=-1, keepdims=True), 1e-20)


def project(h, pos, w_in, g_sgu, g_q, g_k_cmp, g_k_slc, g_k_win):
    B, S = h.shape[:2]
    sizes = [MIX_A, MIX_A, MIX_B] + [KV_W] * 6
    cuts = [sum(sizes[:i + 1]) for i in range(len(sizes))]
    u, v, q, kc, vc, ks, vs, kw, vw, gl = jnp.split(h @ w_in, cuts, axis=-1)
    u = jax.nn.gelu(u).reshape(B, S, A_GROUPS, A_DG)
    v = rmsnorm(jax.nn.gelu(v).reshape(B, S, A_GROUPS, A_DG), g_sgu.reshape(A_GROUPS, A_DG))
    hd = lambda a, n: a.reshape(B, S, n, HEAD_DIM)
    q = rope(rmsnorm(hd(q, N_HEADS), g_q), pos)
    kc = rope(rmsnorm(hd(kc, N_KV), g_k_cmp), pos)
    ks = rope(rmsnorm(hd(ks, N_KV), g_k_slc), pos)
    kw = rope(rmsnorm(hd(kw, N_KV), g_k_win), pos)
    gates = jax.nn.sigmoid(gl.reshape(B, S, N_HEADS, 3))
    return u, v, q, kc, hd(vc, N_KV), ks, hd(vs, N_KV), kw, hd(vw, N_KV), gates


def chunk_mlp(u, v, w_s, b_s):
    B, S = u.shape[:2]
    n_c = -(-S // CHUNK)
    vp = jnp.pad(v, ((0, 0), (0, n_c * CHUNK - S), (0, 0), (0, 0)))
    vp = vp.reshape(B, n_c, CHUNK, A_GROUPS, A_DG)
    w = jnp.where(jnp.tril(jnp.ones((CHUNK, CHUNK), bool)), w_s, 0)
    mixed = jnp.einsum('gts,bcsgd->bctgd', w, vp) + b_s.T[None, None, :, :, None]
    mixed = mixed.reshape(B, n_c * CHUNK, A_GROUPS, A_DG)[:, :S]
    return (u * mixed).reshape(B, S, MIX_A)


def compress(k, pe, w1, w2):
    B, T = k.shape[:2]
    nc = (T - L_CMP) // STRIDE + 1
    idx = jnp.arange(nc)[:, None] * STRIDE + jnp.arange(L_CMP)[None, :]
    blk = k[:, idx] + pe[None, None, :, None, :]
    blk = blk.transpose(0, 1, 3, 2, 4).reshape(B, nc, N_KV, L_CMP * HEAD_DIM)
    return jax.nn.gelu(blk @ w1) @ w2


def to_blocks(k):
    B, T = k.shape[:2]
    n_s = -(-T // L_SLC)
    k = jnp.pad(k, ((0, 0), (0, n_s * L_SLC - T), (0, 0), (0, 0)))
    return k.reshape(B, n_s, L_SLC, N_KV, HEAD_DIM).transpose(0, 3, 1, 2, 4)


def nsa_context(kc, vc, ks, vs, pe_k, pe_v, w_ck1, w_ck2, w_cv1, w_cv2):
    kc_c = compress(kc, pe_k, w_ck1, w_ck2)
    vc_c = compress(vc, pe_v, w_cv1, w_cv2)
    c_end = jnp.arange(kc_c.shape[1]) * STRIDE + (L_CMP - 1)
    return kc_c, vc_c, c_end, to_blocks(ks), to_blocks(vs)


def nsa_core(q, gates, t, kc_c, vc_c, c_end, ks_blk, vs_blk, kw, vw, w_pos):
    B, Q = q.shape[:2]
    f32 = jnp.float32
    qg = q.astype(f32).reshape(B, Q, N_KV, Q_PER_KV, HEAD_DIM) * (HEAD_DIM ** -0.5)
    s_c = jnp.einsum('bqgrd,bcgd->bqgrc', qg, kc_c.astype(f32))
    p_c = masked_probs(s_c, (c_end[None, :] <= t[:, None])[None, :, None, None, :])
    o_c = jnp.einsum('bqgrc,bcgd->bqgrd', p_c, vc_c.astype(f32))
    n_c, n_s = c_end.shape[0], ks_blk.shape[2]
    c_start = jnp.arange(n_c) * STRIDE
    blk = jnp.arange(n_s)
    overlap = ((c_start[:, None] < (blk[None, :] + 1) * L_SLC)
               & (c_start[:, None] + L_CMP > blk[None, :] * L_SLC)).astype(f32)
    imp = jnp.einsum('bqgrc,cn->bqgn', p_c, overlap)
    cur = t // L_SLC
    valid = blk[None, :] <= cur[:, None]
    forced = valid & ((blk[None, :] == 0) | (blk[None, :] >= cur[:, None] - 1))
    score = jnp.where(forced[None, :, None, :], FORCED_SCORE,
                      jnp.where(valid[None, :, None, :], imp, -jnp.inf))
    _, idx = lax.top_k(score, min(N_SEL, n_s))
    bi = jnp.arange(B)[:, None, None, None]
    gi = jnp.arange(N_KV)[None, None, :, None]
    n_k = idx.shape[-1] * L_SLC
    kb = ks_blk[bi, gi, idx].reshape(B, Q, N_KV, n_k, HEAD_DIM).astype(f32)
    vb = vs_blk[bi, gi, idx].reshape(B, Q, N_KV, n_k, HEAD_DIM).astype(f32)
    tok = (idx[..., None] * L_SLC + jnp.arange(L_SLC)).reshape(B, Q, N_KV, n_k)
    s_s = jnp.einsum('bqgrd,bqgkd->bqgrk', qg, kb)
    p_s = masked_probs(s_s, (tok <= t[None, :, None, None])[:, :, :, None, :])
    o_s = jnp.einsum('bqgrk,bqgkd->bqgrd', p_s, vb)
    s_w = jnp.einsum('bqgrd,bkgd->bqgrk', qg, kw.astype(f32))
    rel = t[:, None] - w_pos[None, :]
    m_w = (rel >= 0) & (rel < WINDOW) & (w_pos[None, :] >= 0)
    p_w = masked_probs(s_w, m_w[None, :, None, None, :])
    o_w = jnp.einsum('bqgrk,bkgd->bqgrd', p_w, vw.astype(f32))
    g = gates.astype(f32).reshape(B, Q, N_KV, Q_PER_KV, 3)
    o = g[..., 0:1] * o_c + g[..., 1:2] * o_s + g[..., 2:3] * o_w
    return o.reshape(B, Q, MIX_B).astype(q.dtype)


def nsa_prompt(q, gates, kc_c, vc_c, c_end, ks_blk, vs_blk, kw, vw):
    B, S = q.shape[:2]
    pad = ((0, 0), (WINDOW, 0), (0, 0), (0, 0))
    kw_pad, vw_pad = jnp.pad(kw, pad), jnp.pad(vw, pad)

    def block(qs):
        t = qs + jnp.arange(Q_BLOCK)
        qb = lax.dynamic_slice_in_dim(q, qs, Q_BLOCK, axis=1)
        gb = lax.dynamic_slice_in_dim(gates, qs, Q_BLOCK, axis=1)
        kwb = lax.dynamic_slice_in_dim(kw_pad, qs, WINDOW + Q_BLOCK, axis=1)
        vwb = lax.dynamic_slice_in_dim(vw_pad, qs, WINDOW + Q_BLOCK, axis=1)
        w_pos = qs - WINDOW + jnp.arange(WINDOW + Q_BLOCK)
        return nsa_core(qb, gb, t, kc_c, vc_c, c_end, ks_blk, vs_blk, kwb, vwb, w_pos)

    out = lax.map(block, jnp.arange(S // Q_BLOCK) * Q_BLOCK)
    return out.transpose(1, 0, 2, 3).reshape(B, S, MIX_B)


def gather_pages(cache, page_table):
    rows = cache[page_table]
    return rows.reshape(page_table.shape[0], -1, N_KV, HEAD_DIM)


def finish(x, mix, gate1, shift2, scale2, gate2, w_out, g_ffn_norm, w_ffn_in, w_ffn_out):
    x = x + gate1[:, None] * (mix @ w_out)
    h = modulate(x, g_ffn_norm, shift2, scale2)
    a, b = jnp.split(h @ w_ffn_in, 2, axis=-1)
    return x + gate2[:, None] * ((jax.nn.silu(a) * b) @ w_ffn_out)


def setup_inputs(seed: int = 0) -> dict:
    key = jax.random.key(seed)
    k = jax.random.split(key, 32)
    n_pages = PAST_LEN // PAGE_SIZE
    n_pool = (DEC_BATCH * n_pages * 5) // 4
    wb = min(WINDOW, PAST_LEN)
    f32 = jnp.float32

    def nrm(i, shape, scale=1.0):
        return jax.random.normal(k[i], shape, f32) * scale

    def gain(i, shape):
        return 1.0 + nrm(i, shape, 0.05)

    page_table = jax.random.permutation(k[0], n_pool)[:DEC_BATCH * n_pages]
    page_table = page_table.reshape(DEC_BATCH, n_pages).astype(jnp.int32)
    cache_shape = (DEPTH, n_pool, PAGE_SIZE, N_KV, HEAD_DIM)
    win_shape = (DEPTH, DEC_BATCH, wb, N_KV, HEAD_DIM)
    return {
        'x_prompt': nrm(1, (BATCH, SEQ, D_MODEL)),
        'x_sample': nrm(2, (DEC_BATCH, DEC_SEQ, D_MODEL)),
        'cache_k_cmp': nrm(3, cache_shape),
        'cache_v_cmp': nrm(4, cache_shape),
        'cache_k_slc': nrm(5, cache_shape),
        'cache_v_slc': nrm(6, cache_shape),
        'state_k_win': nrm(7, win_shape),
        'state_v_win': nrm(8, win_shape),
        'page_table': page_table,
        'c_prompt': nrm(9, (BATCH, D_MODEL)),
        'c_sample': nrm(10, (DEC_BATCH, D_MODEL)),
        'w_ada': nrm(11, (DEPTH, D_MODEL, 6 * D_MODEL), 0.5 * D_MODEL ** -0.5),
        'b_ada': nrm(12, (DEPTH, 6 * D_MODEL), 0.1),
        'g_mix_norm': gain(13, (DEPTH, D_MODEL)),
        'g_ffn_norm': gain(14, (DEPTH, D_MODEL)),
        'w_in': nrm(15, (DEPTH, D_MODEL, IN_COLS), D_MODEL ** -0.5),
        'g_sgu': gain(16, (DEPTH, MIX_A)),
        'w_sgu': nrm(17, (DEPTH, A_GROUPS, CHUNK, CHUNK), CHUNK ** -0.5),
        'b_sgu': 1.0 + nrm(18, (DEPTH, A_GROUPS, CHUNK), 0.1),
        'g_q': gain(19, (DEPTH, HEAD_DIM)),
        'g_k_cmp': gain(20, (DEPTH, HEAD_DIM)),
        'g_k_slc': gain(21, (DEPTH, HEAD_DIM)),
        'g_k_win': gain(22, (DEPTH, HEAD_DIM)),
        'pe_k_cmp': nrm(23, (DEPTH, L_CMP, HEAD_DIM), 0.1),
        'pe_v_cmp': nrm(24, (DEPTH, L_CMP, HEAD_DIM), 0.1),
        'w_ck1': nrm(25, (DEPTH, L_CMP * HEAD_DIM, CMP_HID), (L_CMP * HEAD_DIM) ** -0.5),
        'w_ck2': nrm(26, (DEPTH, CMP_HID, HEAD_DIM), CMP_HID ** -0.5),
        'w_cv1': nrm(27, (DEPTH, L_CMP * HEAD_DIM, CMP_HID), (L_CMP * HEAD_DIM) ** -0.5),
        'w_cv2': nrm(28, (DEPTH, CMP_HID, HEAD_DIM), CMP_HID ** -0.5),
        'w_out': nrm(29, (DEPTH, MIX_WIDTH, D_MODEL), MIX_WIDTH ** -0.5),
        'w_ffn_in': nrm(30, (DEPTH, D_MODEL, 2 * D_FF), D_MODEL ** -0.5),
        'w_ffn_out': nrm(31, (DEPTH, D_FF, D_MODEL), D_FF ** -0.5),
    }


def reference(x_prompt, x_sample, cache_k_cmp, cache_v_cmp, cache_k_slc, cache_v_slc,
              state_k_win, state_v_win, page_table, c_prompt, c_sample,
              w_ada, b_ada, g_mix_norm, g_ffn_norm, w_in, g_sgu, w_sgu, b_sgu,
              g_q, g_k_cmp, g_k_slc, g_k_win, pe_k_cmp, pe_v_cmp,
              w_ck1, w_ck2, w_cv1, w_cv2, w_out, w_ffn_in, w_ffn_out):
    B, S = x_prompt.shape[:2]
    n_new = x_sample.shape[1]
    pos_p = jnp.arange(S)
    pos_s = PAST_LEN + jnp.arange(n_new)
    wb_p = min(WINDOW, S)
    wb_s = state_k_win.shape[2]
    open_p = S - ((S - 1) // CHUNK) * CHUNK
    xp, xs = x_prompt, x_sample
    sp = [[] for _ in range(7)]
    ss = [[] for _ in range(7)]
    for l in range(DEPTH):
        proj_w = (w_in[l], g_sgu[l], g_q[l], g_k_cmp[l], g_k_slc[l], g_k_win[l])
        cmp_w = (pe_k_cmp[l], pe_v_cmp[l], w_ck1[l], w_ck2[l], w_cv1[l], w_cv2[l])
        ffn_w = (w_out[l], g_ffn_norm[l], w_ffn_in[l], w_ffn_out[l])

        sh1, sc1, gt1, sh2, sc2, gt2 = jnp.split(jax.nn.silu(c_prompt) @ w_ada[l] + b_ada[l], 6, axis=-1)
        h = modulate(xp, g_mix_norm[l], sh1, sc1)
        u, v, q, kc, vc, ks, vs, kw, vw, gates = project(h, pos_p, *proj_w)
        a_out = chunk_mlp(u, v, w_sgu[l], b_sgu[l])
        ctx = nsa_context(kc, vc, ks, vs, *cmp_w)
        b_out = nsa_prompt(q, gates, *ctx, kw, vw)
        xp = finish(xp, jnp.concatenate([a_out, b_out], axis=-1), gt1, sh2, sc2, gt2, *ffn_w)
        for lst, val in zip(sp, (kc, vc, ks, vs, kw[:, S - wb_p:], vw[:, S - wb_p:],
                                 v.reshape(B, S, MIX_A)[:, S - open_p:])):
            lst.append(val)

        sh1, sc1, gt1, sh2, sc2, gt2 = jnp.split(jax.nn.silu(c_sample) @ w_ada[l] + b_ada[l], 6, axis=-1)
        h = modulate(xs, g_mix_norm[l], sh1, sc1)
        u, v, q, kc, vc, ks, vs, kw, vw, gates = project(h, pos_s, *proj_w)
        a_out = chunk_mlp(u, v, w_sgu[l], b_sgu[l])
        past = lambda cache: gather_pages(cache[l], page_table)
        ctx = nsa_context(jnp.concatenate([past(cache_k_cmp), kc], axis=1),
                          jnp.concatenate([past(cache_v_cmp), vc], axis=1),
                          jnp.concatenate([past(cache_k_slc), ks], axis=1),
                          jnp.concatenate([past(cache_v_slc), vs], axis=1), *cmp_w)
        kw_all = jnp.concatenate([state_k_win[l], kw], axis=1)
        vw_all = jnp.concatenate([state_v_win[l], vw], axis=1)
        w_pos = PAST_LEN - wb_s + jnp.arange(wb_s + n_new)
        b_out = nsa_core(q, gates, pos_s, *ctx, kw_all, vw_all, w_pos)
        xs = finish(xs, jnp.concatenate([a_out, b_out], axis=-1), gt1, sh2, sc2, gt2, *ffn_w)
        for lst, val in zip(ss, (kc, vc, ks, vs, kw_all[:, n_new:], vw_all[:, n_new:],
                                 v.reshape(xs.shape[0], n_new, MIX_A))):
            lst.append(val)

    p_k_cmp, p_v_cmp, p_k_slc, p_v_slc, p_k_win, p_v_win, p_chunk_v = [jnp.stack(a) for a in sp]
    s_k_cmp, s_v_cmp, s_k_slc, s_v_slc, s_k_win, s_v_win, s_chunk_v = [jnp.stack(a) for a in ss]
    return (xp, xs, p_k_cmp, p_v_cmp, p_k_slc, p_v_slc, p_k_win, p_v_win, p_chunk_v,
            s_k_cmp, s_v_cmp, s_k_slc, s_v_slc, s_k_win, s_v_win, s_chunk_v)
```

```python
import contextlib
import bisect
import numpy as np
import ml_dtypes
import concourse.bass as bass
import concourse.mybir as mybir
from concourse.bass_utils import run_bass_kernel_spmd

F32 = mybir.dt.float32
BF16 = mybir.dt.bfloat16
I32 = mybir.dt.int32
AF = mybir.ActivationFunctionType
ALU = mybir.AluOpType
AX = mybir.AxisListType

D = 1024
NEG = -30000.0


class Cfg:
    def __init__(self, SEQ=8192, NS=16, NPOOL=2560, NCORES=8):
        self.SEQ, self.NS, self.NPOOL, self.NCORES = SEQ, NS, NPOOL, NCORES
        self.stop = None


class T:
    __slots__ = ("ap", "w", "r", "name")

    def __init__(self, ap, name=""):
        self.ap, self.w, self.r, self.name = ap, None, {}, name


class TV:
    def __init__(self, parent, ap, name=""):
        self.p, self.ap, self.name = parent, ap, name

    w = property(lambda s: s.p.w, lambda s, v: setattr(s.p, "w", v))
    r = property(lambda s: s.p.r, lambda s, v: setattr(s.p, "r", v))


class Sched:
    R = 6

    def __init__(self, nc, es):
        self.nc = nc
        self.e = {"pe": nc.tensor, "act": nc.scalar, "dve": nc.vector, "pool": nc.gpsimd, "sp": nc.sync}
        self.sem = {k: es.enter_context(nc.semaphore("s_" + k)) for k in ["pe", "act", "dve", "pool"]}
        self.cnt = {k: 0 for k in self.sem}
        self.known = {k: {} for k in self.e}
        self.hist = {k: ([0], [{}]) for k in self.sem}
        self.dirty = {k: False for k in self.e}
        self.dq = {}
        for q in ["sp", "pool"]:
            self.dq[q] = dict(sems=[es.enter_context(nc.semaphore(f"d_{q}{i}")) for i in range(self.R)], n=0)
        self.nwait = 0

    def semof(self, key):
        if isinstance(key, str):
            return self.sem[key]
        return self.dq[key[1]]["sems"][key[2]]

    def _need(self, eng, ev):
        key, val = ev
        kn = self.known[eng]
        if kn.get(key, 0) >= val:
            return
        self.e[eng].wait_ge(self.semof(key), val)
        self.nwait += 1
        kn[key] = val
        self.dirty[eng] = True
        if isinstance(key, str):
            idxs, snaps = self.hist[key]
            i = bisect.bisect_right(idxs, val) - 1
            for k2, v2 in snaps[i].items():
                if kn.get(k2, 0) < v2:
                    kn[k2] = v2

    def _deps(self, eng, reads, writes):
        for t in reads:
            if t.w is not None:
                if t.w[0] == eng and eng in ("pe", "sp"):
                    continue
                self._need(eng, t.w)
        for t in writes:
            if t.w is not None and not (t.w[0] == eng and eng in ("pe", "sp")):
                self._need(eng, t.w)
            for ev in t.r.values():
                if not (ev[0] == eng and eng in ("pe", "sp")):
                    self._need(eng, ev)

    def op(self, eng, fn, reads=(), writes=()):
        self._deps(eng, reads, writes)
        ins = fn(self.e[eng])
        self.cnt[eng] += 1
        n = self.cnt[eng]
        ins.then_inc(self.sem[eng], 1)
        ev = (eng, n)
        if self.dirty[eng]:
            idxs, snaps = self.hist[eng]
            idxs.append(n)
            snaps.append(dict(self.known[eng]))
            self.dirty[eng] = False
        for t in reads:
            t.r[eng] = ev
        for t in writes:
            t.w = ev
            t.r = {}
        return ev

    def dma(self, q, fn, reads=(), writes=()):
        self._deps(q, reads, writes)
        d = self.dq[q]
        i = d["n"] % self.R
        val = 16 * (d["n"] // self.R + 1)
        key = ("d", q, i)
        if d["n"] >= self.R:
            self._need(q, (key, val - 16))
        ins = fn(self.e[q])
        ins.then_inc(d["sems"][i], 16)
        d["n"] += 1
        ev = (key, val)
        for t in reads:
            t.r[key] = ev
        for t in writes:
            t.w = ev
            t.r = {}
        return ev

    def barrier(self):
        evs = [(k, self.cnt[k]) for k in self.sem if self.cnt[k] > 0]
        for q, d in self.dq.items():
            for i in range(self.R):
                if d["n"] > i:
                    evs.append((("d", q, i), 16 * ((d["n"] - 1 - i) // self.R + 1)))
        for eng in self.e:
            for ev in evs:
                if ev[0] != eng:
                    self._need(eng, ev)


class _Stop(Exception):
    pass


def build(cfg):
    holder = {}
    try:
        return _build(cfg, holder)
    except _Stop:
        holder["S"].barrier()
        return holder["nc"]


def _build(cfg, holder):
    SEQ, NS, NPOOL = cfg.SEQ, cfg.NS, cfg.NPOOL
    NT = SEQ // 128
    NOWN = NT // 4
    NSB = SEQ // 64
    NCB = (SEQ - 32) // 16 + 1
    NCT = (NCB + 127) // 128
    NCP = NCT * 128
    NTOK = NS * 4
    NM = SEQ // 256
    NR = NS + 1
    NFY = NSB + 8 * (NOWN - 1)
    NSBS = 33

    nc = bass.Bass("TRN2", target_bir_lowering=False)
    es0 = contextlib.ExitStack()
    S = Sched(nc, es0)
    holder["S"], holder["nc"] = S, nc

    def chk(name):
        if cfg.stop == name:
            raise _Stop()

    def din(name, shape, dt=F32):
        return nc.dram_tensor(name, list(shape), dt, kind="ExternalInput").ap()

    def dout(name, shape, dt=F32):
        return nc.dram_tensor(name, list(shape), dt, kind="ExternalOutput").ap()

    def dscr(name, shape, dt=F32):
        return nc.dram_tensor(name, list(shape), dt, kind="ExternalOutput").ap()

    xb = din("xb", [SEQ, D])
    xown = din("xown", [NOWN * 128, D])
    xs_d = din("xs", [NTOK, D])
    cvec = din("cvec", [NR, D])
    caches = [din(n, [NPOOL * 128, 256]) for n in ("ckc", "cvc", "cks", "cvs")]
    kwin = din("kwin", [NS, 512, 256])
    vwin = din("vwin", [NS, 512, 256])
    ptab = din("ptab", [1, NS * 16], I32)
    w_ada = din("w_ada", [D, 6 * D])
    b_ada = din("b_ada", [1, 6 * D])
    g_mix = din("g_mix", [1, D])
    g_ffn = din("g_ffn", [1, D])
    w_kv = din("w_kv", [D, 1536])
    w_q = din("w_q", [D, 1560])
    w_sT = din("w_sT", [128, 8, 128])
    w_s4 = din("w_s4", [64, 8, 64])
    b_sb = din("b_sb", [128, 512])
    b_sbs = din("b_sbs", [64, 512])
    c_tril = din("c_tril", [128, 128])
    c_m4 = din("c_m4", [64, 64])
    g_sgu = din("g_sgu", [1, 512])
    gains = din("gains", [1, 1280])
    pe2 = din("pe2", [128, 32])
    w_c1 = [din("w_ck1", [2048, 256]), din("w_cv1", [2048, 256])]
    w_c2 = [din("w_ck2", [256, 64]), din("w_cv2", [256, 64])]
    w_out = din("w_out", [D, D])
    w_f1 = din("w_f1", [D, 5632])
    w_f2 = din("w_f2", [2816, D])
    rope_all = din("rope_all", [SEQ, 64])
    rope_own = din("rope_own", [NOWN * 128, 64])
    rope_s = din("rope_s", [NTOK, 64])
    c_ident = din("c_ident", [128, 128], BF16)
    c_ovl = din("c_ovl", [128, NCT, NSB], BF16)
    c_ovls = din("c_ovls", [128, NSBS], BF16)
    c_gjb = din("c_gjb", [128, 20, 128], BF16)
    c_fj = din("c_fj", [128, NFY])
    c_fs = din("c_fs", [64, NSBS])
    c_cm = din("c_cm", [128, 4, 128], BF16)
    c_wm = din("c_wm", [128, 8, 128], BF16)
    c_sel = din("c_sel", [NR, 192])
    c_bd = din("c_bd", [64, 64], BF16)
    c_w0 = din("c_w0", [64, 128], BF16)
    c_iota = din("c_iota", [128, 2])
    c_e32 = din("c_e32", [128, 16, 128], BF16)
    c_e64 = din("c_e64", [128, 32, 128], BF16)

    yown = dout("yown", [NOWN * 128, D])
    ys = dout("ys", [NTOK, D])
    pk = [dout(n, [SEQ, 256]) for n in ("pkc", "pvc", "pks", "pvs")]
    pkw = dout("pkw", [512, 256])
    pvw = dout("pvw", [512, 256])
    pchunk = dout("pchunk", [128, 512])
    sk = [dout(n, [NTOK, 256]) for n in ("skc", "svc", "sks", "svs")]
    skw = dout("skw", [NS, 512, 256])
    svw = dout("svw", [NS, 512, 256])
    schunk = dout("schunk", [NTOK, 512])

    kwT_d = dscr("kwT_d", [NT, 128, 256], BF16)
    vw_d = dscr("vw_d", [NT, 128, 260], BF16)
    x1_d = dscr("x1_d", [NOWN * 128 + NTOK, D])
    ada_d = dscr("ada_d", [NR, 6 * D])

    uid = [0]

    def sb(es, name, shape, dt=F32):
        uid[0] += 1
        return T(es.enter_context(nc.sbuf_tensor(f"{name}_{uid[0]}", list(shape), dt))[:], name)

    def close_scope(es):
        S.barrier()
        es.close()

    ps = [T(es0.enter_context(nc.psum_tensor(f"ps{i}", [128, 512], F32))[:], f"ps{i}") for i in range(7)]
    pst = T(es0.enter_context(nc.psum_tensor("pst", [128, 1024], BF16))[:], "pst")

    def mm(out_t, out_ap, lt, l_ap, rt, r_ap, start, stop):
        rd = [lt] if rt is lt else [lt, rt]
        S.op("pe", lambda e: e.matmul(out_ap, l_ap, r_ap, start=start, stop=stop, skip_group_check=True), reads=rd, writes=[out_t])

    def tr(out_ap, in_t, in_ap, npart):
        S.op("pe", lambda e: e.transpose(out_ap, in_ap, ident.ap[:npart, :npart]), reads=[in_t, ident], writes=[pst])

    def ld(out_t, out_ap, in_ap, reads=(), q="sp"):
        S.dma(q, lambda e: e.dma_start(out=out_ap, in_=in_ap), reads=list(reads), writes=[out_t])

    def st(out_ap, in_t, in_ap, writes=(), q="sp"):
        S.dma(q, lambda e: e.dma_start(out=out_ap, in_=in_ap), reads=[in_t], writes=list(writes))

    def act(func, out_t, out_ap, in_t, in_ap, extra=(), **kw):
        S.op("act", lambda e: e.activation(out=out_ap, in_=in_ap, func=func, **kw), reads=[in_t] + list(extra), writes=[out_t])

    def cp(eng, out_t, out_ap, in_t, in_ap):
        if eng == "act":
            S.op("act", lambda e: e.copy(out=out_ap, in_=in_ap), reads=[in_t], writes=[out_t])
        else:
            S.op(eng, lambda e: e.tensor_copy(out=out_ap, in_=in_ap), reads=[in_t], writes=[out_t])

    def tt(eng, op, out_t, out_ap, a_t, a_ap, b_t, b_ap):
        S.op(eng, lambda e: e.tensor_tensor(out=out_ap, in0=a_ap, in1=b_ap, op=op), reads=[a_t, b_t], writes=[out_t])

    def tsc(eng, out_t, out_ap, a_t, a_ap, s1, s2, op0, op1=None, extra=()):
        if op1 is None:
            S.op(eng, lambda e: e.tensor_scalar(out=out_ap, in0=a_ap, scalar1=s1, scalar2=None, op0=op0), reads=[a_t] + list(extra), writes=[out_t])
        else:
            S.op(eng, lambda e: e.tensor_scalar(out=out_ap, in0=a_ap, scalar1=s1, scalar2=s2, op0=op0, op1=op1), reads=[a_t] + list(extra), writes=[out_t])

    def stt(eng, out_t, out_ap, a_t, a_ap, sc_t, sc, b_t, b_ap, op0, op1):
        rd = [a_t, b_t] + ([sc_t] if sc_t is not None else [])
        S.op(eng, lambda e: e.scalar_tensor_tensor(out=out_ap, in0=a_ap, scalar=sc, in1=b_ap, op0=op0, op1=op1), reads=rd, writes=[out_t])

    def memset(eng, t, ap, val):
        S.op(eng, lambda e: e.memset(ap, val), writes=[t])

    ident = sb(es0, "ident", [128, 128], BF16)
    ld(ident, ident.ap, c_ident)
    epsc = sb(es0, "epsc", [128, 1])
    memset("dve", epsc, epsc.ap, 1e-6)
    selc = sb(es0, "selc", [NR, 192])
    ld(selc, selc.ap, c_sel)
    scr = h32 = hb = ssq = rstd = None
    mod = [None] * 3
    stage = [None] * 2

    def alloc_scratch(es):
        nonlocal scr, h32, hb, ssq, rstd
        scr = sb(es, "scr", [128, D])
        h32 = sb(es, "h32", [128, D])
        hb = sb(es, "hb", [128, D], BF16)
        ssq = sb(es, "ssq", [128, 1])
        rstd = sb(es, "rstd", [128, 1])
        for i in range(3):
            mod[i] = sb(es, f"mod{i}", [128, D])
        for i in range(2):
            stage[i] = sb(es, f"stage{i}", [128, 1024])

    stage_i = [0]
    cast_i = [0]

    def load_cast(dst_t, dst_ap, src_ap, npart, ncol):
        for c0 in range(0, ncol, 1024):
            w = min(1024, ncol - c0)
            sg = stage[stage_i[0] % 2]
            stage_i[0] += 1
            ld(sg, sg.ap[:npart, :w], src_ap[:, c0:c0 + w])
            eng = ("act", "pool", "dve")[cast_i[0] % 3]
            cast_i[0] += 1
            cp(eng, dst_t, dst_ap[:, c0:c0 + w], sg, sg.ap[:npart, :w])

    def load_w(dst_t, src, K, N):
        for k in range(K):
            load_cast(dst_t, dst_t.ap[:, k, :], src[k * 128:(k + 1) * 128, :], 128, N)

    def norm_mod(xt, x_ap, npart, gs, sh, hT, hT_ap):
        act(AF.Square, scr, scr.ap[:npart], xt, x_ap)
        S.op("dve", lambda e: e.reduce_sum(out=ssq.ap[:npart], in_=scr.ap[:npart], axis=AX.X), reads=[scr], writes=[ssq])
        act(AF.Sqrt, rstd, rstd.ap[:npart], ssq, ssq.ap[:npart], extra=[epsc], scale=1.0 / D, bias=epsc.ap[:npart])
        S.op("dve", lambda e: e.reciprocal(out=rstd.ap[:npart], in_=rstd.ap[:npart]), reads=[rstd], writes=[rstd])
        stt("dve", h32, h32.ap[:npart], xt, x_ap, rstd, rstd.ap[:npart, 0:1], gs, gs.ap[:npart], ALU.mult, ALU.mult)
        tt("pool", ALU.add, hb, hb.ap[:npart], h32, h32.ap[:npart], sh, sh.ap[:npart])
        for k in range(8):
            tr(pst.ap[:, k * 128:k * 128 + npart], hb, hb.ap[:npart, k * 128:(k + 1) * 128], npart)
        cp("act", hT, hT_ap, pst, pst.ap.rearrange("p (k t) -> p k t", k=8)[:, :, :npart])

    esA = contextlib.ExitStack()
    alloc_scratch(esA)
    cv = sb(esA, "cv", [NR, D])
    cvb = sb(esA, "cvb", [NR, D], BF16)
    cT = sb(esA, "cT", [128, 8, NR], BF16)
    wch = sb(esA, "wch", [128, 8, 512], BF16)
    bch = sb(esA, "bch", [NR, 512])
    adar = sb(esA, "adar", [NR, 512])
    ld(cv, cv.ap, cvec)
    act(AF.Silu, cv, cv.ap, cv, cv.ap)
    cp("dve", cvb, cvb.ap, cv, cv.ap)
    for k in range(8):
        tr(pst.ap[:, k * 128:k * 128 + NR], cvb, cvb.ap[:, k * 128:(k + 1) * 128], NR)
    cp("act", cT, cT.ap, pst, pst.ap.rearrange("p (k t) -> p k t", k=8)[:, :, :NR])
    ada_t = [T(ada_d[:, c * 512:(c + 1) * 512], f"ada{c}") for c in range(12)]
    for c in range(12):
        for k in range(8):
            load_cast(wch, wch.ap[:, k, :], w_ada[k * 128:(k + 1) * 128, c * 512:(c + 1) * 512], 128, 512)
        ld(bch, bch.ap, b_ada[0:1, c * 512:(c + 1) * 512].partition_broadcast(NR))
        p = ps[c % 2]
        for k in range(8):
            mm(p, p.ap[:NR, :], cT, cT.ap[:, k, :], wch, wch.ap[:, k, :], k == 0, k == 7)
        tt("dve", ALU.add, adar, adar.ap, p, p.ap[:NR, :], bch, bch.ap)
        st(ada_t[c].ap, adar, adar.ap, writes=[ada_t[c]])
    close_scope(esA)
    if cfg.stop == "p0":
        S.barrier()
        return nc

    def load_mod(which, sample):
        esM = contextlib.ExitStack()
        rows = sb(esM, "adarows", [NR, D])
        gbc = sb(esM, "gbc", [128, D])
        npart = NTOK if sample else 128
        sel_ap = selc.ap[:, 128:128 + NTOK] if sample else selc.ap[:, 0:128]
        ld(gbc, gbc.ap, (g_ffn if which else g_mix)[0:1, :].partition_broadcast(128))
        for part, dst in ((1, 0), (0, 1), (2, 2)):
            for c in range(2):
                cc = (which * 3 * D + part * D) // 512 + c
                ld(rows, rows.ap[:, c * 512:(c + 1) * 512], ada_t[cc].ap, reads=[ada_t[cc]])
            for hf in range(2):
                p = ps[hf]
                mm(p, p.ap[:npart, :], selc, sel_ap, rows, rows.ap[:, hf * 512:(hf + 1) * 512], True, True)
                if part == 1:
                    stt("dve", mod[dst], mod[dst].ap[:npart, hf * 512:(hf + 1) * 512], p, p.ap[:npart, :], None, 1.0,
                        gbc, gbc.ap[:npart, hf * 512:(hf + 1) * 512], ALU.add, ALU.mult)
                else:
                    cp("act", mod[dst], mod[dst].ap[:npart, hf * 512:(hf + 1) * 512], p, p.ap[:npart, :])
        close_scope(esM)

    esK = contextlib.ExitStack()
    KccT = sb(esK, "KccT", [128, 2, NCP], BF16)
    KwTm = sb(esK, "KwTm", [128, 8, 256], BF16)
    QTp = sb(esK, "QT", [128, 2, 2, 128], BF16)
    QTz = sb(esK, "QTz", [128, 4, 256], BF16)
    memset("pool", QTz, QTz.ap, 0.0)
    KsT_all = esK.enter_context(nc.sbuf_tensor("KsT", [128, 2, SEQ], BF16))
    KsT = [T(KsT_all[:, :, kt * 128:(kt + 1) * 128], f"KsT{kt}") for kt in range(NT)]
    alloc_scratch(esK)
    Vs_all = esK.enter_context(nc.sbuf_tensor("Vs", [128, NT, 4, 65], BF16))
    Vs_whole = T(Vs_all[:], "Vs_whole")
    Vs = [T(Vs_all[:, kt, :, :], f"Vs{kt}") for kt in range(NT)]
    Vcc = sb(esK, "Vcc", [128, NCT, 4, 65], BF16)
    gainqk = sb(esK, "gainqk", [128, 1280])
    ld(gainqk, gainqk.ap, gains[0:1, :].partition_broadcast(128))
    sqk = sb(esK, "sqk", [128, 768])
    ssqh = sb(esK, "ssqh", [128, 20])
    rsh = sb(esK, "rsh", [128, 20])
    kn = sb(esK, "kn", [128, 768])
    kr = sb(esK, "kr", [128, 768])
    rt1 = sb(esK, "rt1", [128, 384])
    rt2 = sb(esK, "rt2", [128, 384])
    krb = sb(esK, "krb", [128, 768], BF16)
    e64 = sb(esK, "e64", [128, 32, 128], BF16)
    ld(e64, e64.ap, c_e64)

    memset("pool", Vs_whole, Vs_all[:, :, :, 64:65], 1.0)
    for kt in range(NT):
        Vs[kt].w = Vs_whole.w
    memset("pool", KccT, KccT.ap, 0.0)
    memset("pool", Vcc, Vcc.ap, 0.0)
    memset("pool", Vcc, Vcc.ap[:, :, :, 64:65], 1.0)

    def knorm_rope(src_t, src_ap, npart, H, g_ap, rope_t, rope_ap, dst_t, dst_ap):
        W = H * 64
        tt("pool", ALU.mult, sqk, sqk.ap[:npart, :W], src_t, src_ap, src_t, src_ap)
        S.op("dve", lambda e: e.reduce_sum(out=ssqh.ap[:npart, :H], in_=sqk.ap[:npart, :W].rearrange("p (h d) -> p h d", d=64), axis=AX.X),
             reads=[sqk], writes=[ssqh])
        act(AF.Sqrt, rsh, rsh.ap[:npart, :H], ssqh, ssqh.ap[:npart, :H], extra=[epsc], scale=1.0 / 64, bias=epsc.ap[:npart])
        S.op("dve", lambda e: e.reciprocal(out=rsh.ap[:npart, :H], in_=rsh.ap[:npart, :H]), reads=[rsh], writes=[rsh])
        kn3 = kn.ap[:npart, :W].rearrange("p (h d) -> p h d", d=64)
        tt("dve", ALU.mult, kn, kn3, src_t, src_ap.rearrange("p (h d) -> p h d", d=64), rsh,
           rsh.ap[:npart, :H].unsqueeze(2).to_broadcast([npart, H, 64]))
        tt("pool", ALU.mult, kn, kn.ap[:npart, :W], kn, kn.ap[:npart, :W], gainqk, g_ap)
        x1 = kn3[:, :, 0:32]
        x2 = kn3[:, :, 32:64]
        cosb = rope_ap[:, 0:32].unsqueeze(1).to_broadcast([npart, H, 32])
        sinb = rope_ap[:, 32:64].unsqueeze(1).to_broadcast([npart, H, 32])
        d3 = dst_ap.rearrange("p (h d) -> p h d", d=64)
        a1 = rt1.ap[:npart, :H * 32].rearrange("p (h d) -> p h d", d=32)
        a2 = rt2.ap[:npart, :H * 32].rearrange("p (h d) -> p h d", d=32)
        tt("dve", ALU.mult, rt1, a1, kn, x1, rope_t, cosb)
        tt("pool", ALU.mult, rt2, a2, kn, x2, rope_t, sinb)
        tt("dve", ALU.subtract, dst_t, d3[:, :, 0:32], rt1, a1, rt2, a2)
        tt("dve", ALU.mult, rt1, a1, kn, x2, rope_t, cosb)
        tt("pool", ALU.mult, rt2, a2, kn, x1, rope_t, sinb)
        tt("pool", ALU.add, dst_t, d3[:, :, 32:64], rt1, a1, rt2, a2)

    chk("a0")
    load_mod(0, False)
    chk("a1")
    esB = contextlib.ExitStack()
    wkv = sb(esB, "wkv", [128, 8, 1536], BF16)
    load_w(wkv, w_kv, 8, 1536)
    chk("a2")
    xbuf = [sb(esB, f"xbuf{i}", [128, D]) for i in range(2)]
    hT = sb(esB, "hT", [128, 8, 128], BF16)
    kv32 = sb(esB, "kv32", [128, 1536])
    vwb = sb(esB, "vwb", [128, 4, 65], BF16)
    kwTt = sb(esB, "kwTt", [128, 256], BF16)
    ropeAs = [sb(esB, f"ropeA{i}", [128, 64]) for i in range(2)]
    memset("pool", vwb, vwb.ap[:, :, 64:65], 1.0)
    pk_t = [[T(pk[i][t * 128:(t + 1) * 128, :], f"pk{i}_{t}") for t in range(NT)] for i in range(4)]
    kwT_t = [T(kwT_d[t], f"kwTd{t}") for t in range(NT)]
    vw_t = [T(vw_d[t], f"vwd{t}") for t in range(NT)]
    out_evs = []

    ld(xbuf[0], xbuf[0].ap, xb[0:128, :])
    for t in range(NT):
        xt = xbuf[t % 2]
        if t + 1 < NT:
            ld(xbuf[(t + 1) % 2], xbuf[(t + 1) % 2].ap, xb[(t + 1) * 128:(t + 2) * 128, :])
        if t == 0:
            chk("a3")
        ropeA = ropeAs[t % 2]
        ld(ropeA, ropeA.ap, rope_all[t * 128:(t + 1) * 128, :])
        norm_mod(xt, xt.ap, 128, mod[0], mod[1], hT, hT.ap)
        if t == 0:
            chk("a4")
        for ch in range(3):
            p = ps[ch % 2]
            for k in range(8):
                mm(p, p.ap, hT, hT.ap[:, k, :], wkv, wkv.ap[:, k, ch * 512:(ch + 1) * 512], k == 0, k == 7)
            cp("act", kv32, kv32.ap[:, ch * 512:(ch + 1) * 512], p, p.ap)
        if t == 0:
            chk("a5")
        knorm_rope(kv32, kv32.ap[:, 0:768], 128, 12, gainqk.ap[:, 512:1280], ropeA, ropeA.ap, kr, kr.ap[:, 0:768])
        if t == 0:
            chk("a6")
        st(pk_t[0][t].ap, kr, kr.ap[:, 0:256], writes=[pk_t[0][t]])
        st(pk_t[2][t].ap, kr, kr.ap[:, 256:512], writes=[pk_t[2][t]])
        st(pk_t[1][t].ap, kv32, kv32.ap[:, 768:1024], writes=[pk_t[1][t]])
        st(pk_t[3][t].ap, kv32, kv32.ap[:, 1024:1280], writes=[pk_t[3][t]])
        if t == 0:
            chk("a61")
        if t >= NT - 4:
            r0 = (t - (NT - 4)) * 128
            out_evs.append(S.dma("sp", lambda e: e.dma_start(out=pkw[r0:r0 + 128, :], in_=kr.ap[:, 512:768]), reads=[kr]))
            out_evs.append(S.dma("sp", lambda e: e.dma_start(out=pvw[r0:r0 + 128, :], in_=kv32.ap[:, 1280:1536]), reads=[kv32]))
        for kti, s0_ in enumerate((256, 512)):
            cp(("act", "dve")[kti], krb, krb.ap[:, kti * 256:(kti + 1) * 256].rearrange("p (s h d) -> p s h d", s=2, h=2),
               kr, kr.ap[:, s0_:s0_ + 256].rearrange("p (h s d) -> p s h d", h=2, s=2))
        if t == 0:
            chk("a62")
        cp("pool", Vs[t], Vs[t].ap[:, :, 0:64], kv32, kv32.ap[:, 1024:1280].rearrange("p (g d) -> p g d", d=64))
        cp("pool", vwb, vwb.ap[:, :, 0:64], kv32, kv32.ap[:, 1280:1536].rearrange("p (g d) -> p g d", d=64))
        if t == 0:
            chk("a63")
        st(vw_t[t].ap, vwb, vwb.ap.rearrange("p g d -> p (g d)"), writes=[vw_t[t]])
        if t == 0:
            chk("a64")
        for i4 in range(4):
            tr(pst.ap[:, i4 * 128:(i4 + 1) * 128], krb, krb.ap[:, i4 * 128:(i4 + 1) * 128], 128)
        if t == 0:
            chk("a65")
        cp("act", KsT[t], KsT[t].ap, pst, pst.ap[:, 0:256].rearrange("p (s t) -> p s t", s=2))
        if t == 0:
            chk("a66")
        cp("act", kwTt, kwTt.ap, pst, pst.ap[:, 256:512])
        if t == 0:
            chk("a67")
        st(kwT_t[t].ap, kwTt, kwTt.ap, writes=[kwT_t[t]])
        if t == 0:
            chk("a7")
    close_scope(esB)
    if cfg.stop == "p1a":
        S.barrier()
        return nc

    def compress_hidden(KT2_t, KT2_ap, G, nblk, w1_t, b1_t, hidT_t, hidT_ap):
        N = G * nblk
        for hh in range(2):
            p = ps[2 + hh]
            o_ap = p.ap[:, :N] if G == 1 else p.ap[:, :N].rearrange("p (g n) -> p g n", g=G)
            for q in range(16):
                r_ap = KT2_ap[:, :, q:q + 8 * (nblk - 1) + 1:8]
                if G == 1:
                    r_ap = KT2_ap[:, 0, q:q + 8 * (nblk - 1) + 1:8]
                mm(p, o_ap, w1_t, w1_t.ap[:, q, hh * 128:(hh + 1) * 128], KT2_t, r_ap, q == 0, q == 15)
            act(AF.Gelu_apprx_tanh, hidT_t, hidT_ap[:, hh, :N], p, p.ap[:, :N], extra=[b1_t], bias=b1_t.ap[:, hh:hh + 1])

    def load_cmp_weights(es, which):
        w1 = sb(es, f"w1_{which}", [128, 16, 256], BF16)
        load_w(w1, w_c1[which], 16, 256)
        w2 = sb(es, f"w2_{which}", [128, 2, 128], BF16)
        for hh in range(2):
            load_cast(w2, w2.ap[:, hh, 0:64], w_c2[which][hh * 128:(hh + 1) * 128, :], 128, 64)
            load_cast(w2, w2.ap[:, hh, 64:128], w_c2[which][hh * 128:(hh + 1) * 128, :], 128, 64)
        pe_f = sb(es, f"pef_{which}", [128, 16])
        ld(pe_f, pe_f.ap, pe2[:, which * 16:(which + 1) * 16])
        pe_b = sb(es, f"peb_{which}", [128, 16], BF16)
        cp("dve", pe_b, pe_b.ap, pe_f, pe_f.ap)
        b1 = sb(es, f"b1_{which}", [128, 2])
        for hh in range(2):
            p = ps[hh]
            for q in range(16):
                mm(p, p.ap[:, 0:1], w1, w1.ap[:, q, hh * 128:(hh + 1) * 128], pe_b, pe_b.ap[:, q:q + 1], q == 0, q == 15)
            cp("dve", b1, b1.ap[:, hh:hh + 1], p, p.ap[:, 0:1])
        return w1, w2, b1

    esC = contextlib.ExitStack()
    cw = [load_cmp_weights(esC, 0), load_cmp_weights(esC, 1)]
    KT2 = [sb(esC, "KT2", [128, 4, SEQ // 2], BF16)] * 2
    rp = [sb(esC, f"rp{i}", [128, 512]) for i in range(2)]
    rpb = [sb(esC, f"rpb{i}", [128, 512], BF16) for i in range(2)]
    hidT = sb(esC, "hidT", [128, 2, 4, NCP], BF16)
    memset("pool", hidT, hidT.ap, 0.0)
    import os as _os
    if "z" in _os.environ.get("DBG", ""):
        dummy = sb(esC, "dummy", [128, int(_os.environ.get("DZ", "4096"))])
        memset("pool", dummy, dummy.ap, 0.0)
        print("sbuf remaining", nc.sbuf_bytes_remaining)
    for which in range(2):
        src2 = pk[which].rearrange("(m r) c -> m (r c)", r=2)
        for mt in range(NM):
            a, bq = rp[mt % 2], rpb[mt % 2]
            ld(a, a.ap, src2[mt * 128:(mt + 1) * 128, :], reads=[pk_t[which][2 * mt], pk_t[which][2 * mt + 1]])
            cp(("act", "pool")[mt % 2], bq, bq.ap.rearrange("p (g r d) -> p g r d", g=4, r=2), a, a.ap.rearrange("p (r g d) -> p g r d", r=2, g=4))
            for g in range(4):
                tr(pst.ap[:, g * 128:(g + 1) * 128], bq, bq.ap[:, g * 128:(g + 1) * 128], 128)
            cp("act", KT2[which], KT2[which].ap[:, :, mt * 128:(mt + 1) * 128], pst,
               pst.ap[:, 0:512].rearrange("p (g t) -> p g t", g=4))
        w1, w2, b1 = cw[which]
        for g in range(4):
            compress_hidden(KT2[which], KT2[which].ap[:, g:g + 1, :], 1, NCB, w1, b1, hidT, hidT.ap[:, :, g, :])
        if which == 0:
            for g in range(4):
                p = ps[4 + g % 2]
                for hh in range(2):
                    mm(p, p.ap[:, :NCB], w2, w2.ap[:, hh, :], hidT, hidT.ap[:, hh, g, :NCB], hh == 0, hh == 1)
                h0 = 64 * (g // 2)
                cp("act", KccT, KccT.ap[h0:h0 + 64, g % 2, :NCB], p, p.ap[h0:h0 + 64, :NCB])
        else:
            for ct in range(NCT):
                p = ps[4 + ct % 2]
                for g in range(4):
                    for hh in range(2):
                        mm(p, p.ap[:, g * 64:(g + 1) * 64], hidT, hidT.ap[:, hh, g, ct * 128:(ct + 1) * 128], w2, w2.ap[:, hh, 0:64],
                           hh == 0, hh == 1)
                cp("act", Vcc, Vcc.ap[:, ct, :, 0:64], p, p.ap[:, 0:256].rearrange("p (g d) -> p g d", d=64))
    close_scope(esC)
    if cfg.stop == "p1c":
        S.barrier()
        return nc

    def load_mix_weights(es, tag):
        wq = sb(es, "wq" + tag, [128, 8, 1560], BF16)
        load_w(wq, w_q, 8, 1560)
        wo = sb(es, "wo" + tag, [128, 8, D], BF16)
        load_w(wo, w_out, 8, D)
        bsb = sb(es, "bsb" + tag, [128, 512])
        ld(bsb, bsb.ap, b_sb)
        gsg = sb(es, "gsg" + tag, [128, 512])
        ld(gsg, gsg.ap, g_sgu[0:1, :].partition_broadcast(128))
        return wq, wo, bsb, gsg

    esW = contextlib.ExitStack()
    wq, wo, bsb, gsg = load_mix_weights(esW, "")
    wsT = sb(esW, "wsT", [128, 8, 128], BF16)
    trl = sb(esW, "trl", [128, 128])
    ld(trl, trl.ap, c_tril)
    for g0 in range(0, 8, 4):
        sg = stage[0]
        ld(sg, sg.ap[:, 0:512], w_sT[:, g0:g0 + 4, :].rearrange("p g t -> p (g t)"))
        tt("dve", ALU.mult, wsT, wsT.ap[:, g0:g0 + 4, :], sg, sg.ap[:, 0:512].rearrange("p (g t) -> p g t", g=4),
           trl, trl.ap.unsqueeze(1).to_broadcast([128, 4, 128]))

    esD = contextlib.ExitStack()
    xbuf = [sb(esD, "xo", [128, D])] * 2
    hT = sb(esD, "hTo", [128, 8, 128], BF16)
    ropeO = sb(esD, "ropeO", [128, 64])
    ovl = sb(esD, "ovl", [128, NCT, NSB], BF16)
    ld(ovl, ovl.ap, c_ovl)
    gjb = sb(esD, "gjb", [128, 20, 128], BF16)
    ld(gjb, gjb.ap, c_gjb)
    fj = sb(esD, "fj", [128, NFY])
    ld(fj, fj.ap, c_fj)
    cmk = sb(esD, "cmk", [128, 4, 128], BF16)
    ld(cmk, cmk.ap, c_cm)
    wmk = sb(esD, "wmk", [128, 8, 128], BF16)
    ld(wmk, wmk.ap, c_wm)
    q32 = TV(h32, h32.ap[:, 0:512], "q32")
    u32 = TV(h32, h32.ap[:, 512:1024], "u32")
    v32 = TV(scr, scr.ap[:, 0:512], "v32")
    vnb = sb(esD, "vnb", [128, 512], BF16)
    ssv = sb(esD, "ssv", [128, 8])
    gat = sb(esD, "gat", [128, 24])
    QT = QTp
    mixb = sb(esD, "mixb", [128, D], BF16)
    mixT = hT
    PT = [sb(esD, f"PT{i}", [128, 256], BF16) for i in range(2)] * 2
    Vwm = sb(esD, "Vwm", [128, 8, 260], BF16)
    rc = sb(esD, "rc", [128, 2])
    imp = sb(esD, "imp", [128, NSB])
    score = sb(esD, "score", [128, NSB])
    sc2 = sb(esD, "sc2", [128, NSB])
    m8a = sb(esD, "m8a", [128, 8])
    m8b = sb(esD, "m8b", [128, 8])
    selb = sb(esD, "selb", [128, NSB], BF16)
    selT = sb(esD, "selT", [128, 2, 128], BF16)
    rcs = sb(esD, "rcs", [128, 2, 3])
    coef = sb(esD, "coef", [128, 2, 3])
    bo32 = TV(stage[1], stage[1].ap[:, 0:512], "bo32")
    x1t = stage[0]
    x1_t = [T(x1_d[i * 128:(i + 1) * 128, :], f"x1d{i}") for i in range(NOWN)]
    x1s_t = T(x1_d[NOWN * 128:NOWN * 128 + NTOK, :], "x1ds")
    STt = [T(ps[2].ap[:, 0:256], "ST0"), T(ps[3].ap[:, 0:256], "ST1")] * 2
    psE, psF, psG = ps[4], ps[5], ps[6]
    st_i = [0]

    def vnorm_gelu(npart, pv, vdst_t, vdst_ap):
        act(AF.Gelu_apprx_tanh, v32, v32.ap[:npart], pv, pv.ap[:npart, :])
        tt("pool", ALU.mult, sqk, sqk.ap[:npart, :512], v32, v32.ap[:npart], v32, v32.ap[:npart])
        S.op("dve", lambda e: e.reduce_sum(out=ssv.ap[:npart], in_=sqk.ap[:npart, :512].rearrange("p (h d) -> p h d", d=64), axis=AX.X),
             reads=[sqk], writes=[ssv])
        act(AF.Sqrt, ssv, ssv.ap[:npart], ssv, ssv.ap[:npart], extra=[epsc], scale=1.0 / 64, bias=epsc.ap[:npart])
        S.op("dve", lambda e: e.reciprocal(out=ssv.ap[:npart], in_=ssv.ap[:npart]), reads=[ssv], writes=[ssv])
        tt("dve", ALU.mult, v32, v32.ap[:npart].rearrange("p (h d) -> p h d", d=64), v32, v32.ap[:npart].rearrange("p (h d) -> p h d", d=64),
           ssv, ssv.ap[:npart].unsqueeze(2).to_broadcast([npart, 8, 64]))
        tt("pool", ALU.mult, vdst_t, vdst_ap, v32, v32.ap[:npart], gsg, gsg.ap[:npart])

    chk("c0")
    for m in range(NOWN):
        xt = xbuf[0]
        ld(xt, xt.ap, xown[m * 128:(m + 1) * 128, :])
        ld(ropeO, ropeO.ap, rope_own[m * 128:(m + 1) * 128, :])
        norm_mod(xt, xt.ap, 128, mod[0], mod[1], hT, hT.ap)
        d0 = max(0, 4 - 4 * m)
        for dl in range(d0, 8):
            kt = 4 * m - 4 + dl
            ld(KwTm, KwTm.ap[:, dl, :], kwT_t[kt].ap, reads=[kwT_t[kt]])
            ld(Vwm, Vwm.ap[:, dl, :], vw_t[kt].ap, reads=[vw_t[kt]])
        pq, pu = ps[0], ps[1]
        for k in range(8):
            mm(pq, pq.ap, hT, hT.ap[:, k, :], wq, wq.ap[:, k, 0:512], k == 0, k == 7)
        cp("act", q32, q32.ap, pq, pq.ap)
        for k in range(8):
            mm(pu, pu.ap, hT, hT.ap[:, k, :], wq, wq.ap[:, k, 512:1024], k == 0, k == 7)
        act(AF.Gelu_apprx_tanh, u32, u32.ap, pu, pu.ap)
        for k in range(8):
            mm(pq, pq.ap, hT, hT.ap[:, k, :], wq, wq.ap[:, k, 1024:1536], k == 0, k == 7)
        vnorm_gelu(128, pq, kn, kn.ap[:, 0:512])
        if m == NOWN - 1:
            out_evs.append(S.dma("sp", lambda e: e.dma_start(out=pchunk, in_=kn.ap[:, 0:512]), reads=[kn]))
        cp("act", vnb, vnb.ap, kn, kn.ap[:, 0:512])
        for k in range(8):
            mm(pu, pu.ap[:, 0:24], hT, hT.ap[:, k, :], wq, wq.ap[:, k, 1536:1560], k == 0, k == 7)
        act(AF.Sigmoid, gat, gat.ap, pu, pu.ap[:, 0:24])
        for g in range(8):
            mm(pq, pq.ap[:, g * 64:(g + 1) * 64], wsT, wsT.ap[:, g, :], vnb, vnb.ap[:, g * 64:(g + 1) * 64], True, True)
        tt("dve", ALU.add, sqk, sqk.ap[:, 0:512], pq, pq.ap, bsb, bsb.ap)
        tt("pool", ALU.mult, mixb, mixb.ap[:, 0:512], sqk, sqk.ap[:, 0:512], u32, u32.ap)
        knorm_rope(q32, q32.ap, 128, 8, gainqk.ap[:, 0:512], ropeO, ropeO.ap, kr, kr.ap[:, 0:512])
        cp("act", krb, krb.ap[:, 0:512].rearrange("p (q h d) -> p q h d", q=4, h=2), kr, kr.ap[:, 0:512].rearrange("p (h q d) -> p q h d", h=2, q=4))
        for i4 in range(4):
            tr(pst.ap[:, i4 * 128:(i4 + 1) * 128], krb, krb.ap[:, i4 * 128:(i4 + 1) * 128], 128)
        cp("act", QT, QT.ap, pst, pst.ap[:, 0:512].rearrange("p (s r t) -> p s r t", s=2, r=2))
        for g_ in range(4):
            p0_ = 64 * (g_ // 2)
            cp(("act", "dve")[g_ % 2], QTz, QTz.ap[p0_:p0_ + 64, g_, :], QT, QT.ap[p0_:p0_ + 64, g_ % 2, :, :].rearrange("p r t -> p (r t)"))

        chk("c1")
        import os as _os
        for g in range(int(_os.environ.get("NG", "4"))):
            P0 = 64 * (g // 2)
            slot = g % 2
            qT_ap = QTz.ap[:, g, :]
            cts = [ct for ct in range(NCT) if 4 * m + 3 - 16 * ct >= 0]
            import os as _os
            if "x" in _os.environ.get("DBG", ""):
                cts = cts[:1]
            first = True
            for ci, ct in enumerate(cts):
                k_ = 4 * m - 16 * ct
                stt_ = STt[st_i[0] % 4]
                ptb = PT[st_i[0] % 4]
                st_i[0] += 1
                need_mask = k_ < 17
                mm(stt_, stt_.ap, KccT, KccT.ap[:, slot, ct * 128:(ct + 1) * 128],
                   QTz, qT_ap, True, not need_mask)
                if need_mask:
                    for r in range(2):
                        mm(stt_, stt_.ap[:, r * 128:(r + 1) * 128], gjb, gjb.ap[:, k_ + 3, :], ident, ident.ap, False, r == 1)
                act(AF.Exp, ptb, ptb.ap, stt_, stt_.ap, scale=0.125)
                for r in range(2):
                    mm(psE, psE.ap[:, r * 65:(r + 1) * 65], ptb, ptb.ap[:, r * 128:(r + 1) * 128], Vcc, Vcc.ap[:, ct, g, :], first, False)
                    first = False
                    mm(psE, psE.ap[:, 130 + r * NSB:130 + (r + 1) * NSB], ptb, ptb.ap[:, r * 128:(r + 1) * 128], ovl, ovl.ap[:, ct, :],
                       False, ci == len(cts) - 1)
            chk("c2")
            tsc("dve", rc, rc.ap, psE, psE.ap[:, 64:130:65], 1e-20, None, ALU.max)
            S.op("dve", lambda e: e.reciprocal(out=rc.ap, in_=rc.ap), reads=[rc], writes=[rc])
            tsc("dve", imp, imp.ap, psE, psE.ap[:, 130:130 + NSB], rc.ap[:, 0:1], None, ALU.mult, extra=[rc])
            stt("dve", imp, imp.ap, psE, psE.ap[:, 130 + NSB:130 + 2 * NSB], rc, rc.ap[:, 1:2], imp, imp.ap, ALU.mult, ALU.add)
            y0 = 8 * (NOWN - 1) - 8 * m
            tt("dve", ALU.add, score, score.ap, imp, imp.ap, fj, fj.ap[:, y0:y0 + NSB])
            tsc("dve", score, score.ap[:, 0:1], score, score.ap[:, 0:1], 1e4, None, ALU.add)
            S.op("dve", lambda e: e.max(out=m8a.ap, in_=score.ap), reads=[score], writes=[m8a])
            S.op("dve", lambda e: e.match_replace(out=sc2.ap, in_to_replace=m8a.ap, in_values=score.ap, imm_value=-3e4),
                 reads=[m8a, score], writes=[sc2])
            S.op("dve", lambda e: e.max(out=m8b.ap, in_=sc2.ap), reads=[sc2], writes=[m8b])
            tsc("dve", sc2, sc2.ap, score, score.ap, m8b.ap[:, 7:8], -NEG, ALU.is_ge, ALU.mult, extra=[m8b])
            tsc("dve", selb, selb.ap, sc2, sc2.ap, NEG, None, ALU.add)
            tr(pst.ap[:NSB, 0:128], selb, selb.ap, 128)
            cp("act", selT, selT.ap[:NSB], pst, pst.ap[:NSB, 0:128].unsqueeze(1).to_broadcast([NSB, 2, 128]))
            chk("c3")
            nkt = 4 * m + 4
            pend = None
            for kt in range(nkt + 1):
                cur = None
                if kt < nkt:
                    stt_ = STt[st_i[0] % 4]
                    ptb = PT[st_i[0] % 4]
                    st_i[0] += 1
                    mm(stt_, stt_.ap, KsT[kt], KsT[kt].ap[:, slot, :], QTz, qT_ap, True, False)
                    diag = kt >= 4 * m
                    a64 = 64 * ((2 * kt) // 64)
                    k64 = min(64, NSB - a64)
                    mm(stt_, stt_.ap, e64, e64.ap[a64:a64 + k64, kt % 32, :], selT, selT.ap[a64:a64 + k64].rearrange("p r t -> p (r t)"), False, not diag)
                    if diag:
                        for r in range(2):
                            mm(stt_, stt_.ap[:, r * 128:(r + 1) * 128], cmk, cmk.ap[:, kt - 4 * m, :], ident, ident.ap, False, r == 1)
                    cur = (stt_, ptb, kt)
                if pend is not None:
                    s_, p_, k2 = pend
                    act(AF.Exp, p_, p_.ap, s_, s_.ap, scale=0.125)
                    for r in range(2):
                        mm(psF, psF.ap[:, r * 65:(r + 1) * 65], p_, p_.ap[:, r * 128:(r + 1) * 128], Vs[k2], Vs[k2].ap[:, g, :],
                           k2 == 0 and r == 0, k2 == nkt - 1)
                pend = cur
            chk("c4")
            firstw = True
            for dl in range(d0, 8):
                stt_ = STt[st_i[0] % 4]
                ptb = PT[st_i[0] % 4]
                st_i[0] += 1
                kw_ap = KwTm.ap[:, dl, :].rearrange("p (s t) -> p s t", s=2)[:, slot, :]
                mm(stt_, stt_.ap, KwTm, kw_ap, QTz, qT_ap, True, False)
                for r in range(2):
                    mm(stt_, stt_.ap[:, r * 128:(r + 1) * 128], wmk, wmk.ap[:, dl, :], ident, ident.ap, False, r == 1)
                act(AF.Exp, ptb, ptb.ap, stt_, stt_.ap, scale=0.125)
                for r in range(2):
                    mm(psG, psG.ap[:, r * 65:(r + 1) * 65], ptb, ptb.ap[:, r * 128:(r + 1) * 128], Vwm,
                       Vwm.ap[:, dl, g * 65:(g + 1) * 65], firstw, dl == 7)
                    firstw = False
            chk("c5")
            cp("dve", rcs, rcs.ap[:, :, 0], rc, rc.ap)
            cp("dve", rcs, rcs.ap[:, :, 1], psF, psF.ap[:, 64:130:65])
            cp("dve", rcs, rcs.ap[:, :, 2], psG, psG.ap[:, 64:130:65])
            S.op("dve", lambda e: e.reciprocal(out=rcs.ap[:, :, 1:3], in_=rcs.ap[:, :, 1:3]), reads=[rcs], writes=[rcs])
            tt("dve", ALU.mult, coef, coef.ap, rcs, rcs.ap, gat, gat.ap[:, 6 * g:6 * g + 6].rearrange("p (r k) -> p r k", r=2))
            for r in range(2):
                h = 2 * g + r
                dst = bo32.ap[:, h * 64:(h + 1) * 64]
                tsc("dve", bo32, dst, psE, psE.ap[:, r * 65:r * 65 + 64], coef.ap[:, r, 0:1], None, ALU.mult, extra=[coef])
                stt("dve", bo32, dst, psF, psF.ap[:, r * 65:r * 65 + 64], coef, coef.ap[:, r, 1:2], bo32, dst, ALU.mult, ALU.add)
                stt("dve", mixb, mixb.ap[:, 512 + h * 64:512 + (h + 1) * 64], psG, psG.ap[:, r * 65:r * 65 + 64], coef, coef.ap[:, r, 2:3],
                    bo32, dst, ALU.mult, ALU.add)
        chk("c7")
        for k in range(8):
            tr(pst.ap[:, k * 128:(k + 1) * 128], mixb, mixb.ap[:, k * 128:(k + 1) * 128], 128)
        cp("act", mixT, mixT.ap, pst, pst.ap.rearrange("p (k t) -> p k t", k=8))
        for hf in range(2):
            p = ps[hf]
            for k in range(8):
                mm(p, p.ap, mixT, mixT.ap[:, k, :], wo, wo.ap[:, k, hf * 512:(hf + 1) * 512], k == 0, k == 7)
            tt("dve", ALU.mult, x1t, x1t.ap[:, hf * 512:(hf + 1) * 512], p, p.ap, mod[2], mod[2].ap[:, hf * 512:(hf + 1) * 512])
        tt("pool", ALU.add, x1t, x1t.ap, x1t, x1t.ap, xt, xt.ap)
        st(x1_t[m].ap, x1t, x1t.ap, writes=[x1_t[m]])
        chk(f"b{m}")
    close_scope(esD)
    close_scope(esW)
    close_scope(esK)
    if cfg.stop == "p1b":
        S.barrier()
        return nc

    esS = contextlib.ExitStack()
    KnT = sb(esS, "KnT_s", [128, 2, 2, 128], BF16)
    KccS = sb(esS, "KccS", [128, 2, 128], BF16)
    QT = sb(esS, "QT_s", [128, 2, 2, NTOK], BF16)
    KsTs = [sb(esS, f"KsTs{i}", [128, 2, 128], BF16) for i in range(2)]
    KwTs = sb(esS, "KwTs", [128, 4, 2, 128], BF16)
    alloc_scratch(esS)
    load_mod(0, True)
    wo = sb(esS, "wo_s", [128, 8, D], BF16)
    load_w(wo, w_out, 8, D)
    cws = [load_cmp_weights(esS, 0), load_cmp_weights(esS, 1)]
    xst = sb(esS, "xst", [NTOK, D])
    gat = sb(esS, "gat_s", [NTOK, 24])
    mixb = sb(esS, "mixb_s", [NTOK, D], BF16)
    mixT = sb(esS, "mixT_s", [128, 8, NTOK], BF16)
    Vn = sb(esS, "Vn_s", [128, 2, 4, 65], BF16)
    memset("pool", KnT, KnT.ap, 0.0)
    memset("pool", Vn, Vn.ap, 0.0)
    fs = sb(esS, "fs", [NTOK, NSBS])
    ld(fs, fs.ap, c_fs[:NTOK, :])
    ovls = sb(esS, "ovls", [128, NSBS], BF16)
    ld(ovls, ovls.ap, c_ovls)
    bdm = sb(esS, "bdm", [NTOK, 128], BF16)
    memset("pool", bdm, bdm.ap, 0.0)
    ld(bdm, bdm.ap[:, :NTOK], c_bd[:NTOK, :NTOK])
    w0m = sb(esS, "w0m", [NTOK, 128], BF16)
    ld(w0m, w0m.ap, c_w0[:NTOK, :])
    idx_tok = sb(esS, "idx_tok", [128, NS * 16], I32)
    idx_pr = sb(esS, "idx_pr", [128, NS * 8], I32)

    esP = contextlib.ExitStack()
    wq = sb(esP, "wq_s", [128, 8, 1560], BF16)
    load_w(wq, w_q, 8, 1560)
    wkv = sb(esP, "wkv_s", [128, 8, 1536], BF16)
    load_w(wkv, w_kv, 8, 1536)
    gsg = sb(esP, "gsg_s", [128, 512])
    ld(gsg, gsg.ap, g_sgu[0:1, :].partition_broadcast(128))
    ws4 = sb(esP, "ws4", [64, 8, 64], BF16)
    sg = stage[0]
    ld(sg, sg.ap[:64, 0:512], w_s4.rearrange("p g t -> p (g t)"))
    m4 = sb(esP, "m4", [64, 64])
    ld(m4, m4.ap, c_m4)
    tt("dve", ALU.mult, ws4, ws4.ap, sg, sg.ap[:64, 0:512].rearrange("p (g t) -> p g t", g=8), m4, m4.ap.unsqueeze(1).to_broadcast([64, 8, 64]))
    bsbs = sb(esP, "bsbs", [64, 512])
    ld(bsbs, bsbs.ap, b_sbs)
    gainqk = sb(esP, "gainqk_s", [128, 1280])
    ld(gainqk, gainqk.ap, gains[0:1, :].partition_broadcast(128))
    sqk = sb(esP, "sqk_s", [128, 1280])
    ssqh = sb(esP, "ssqh_s", [128, 20])
    rsh = sb(esP, "rsh_s", [128, 20])
    kn = sb(esP, "kn_s", [128, 1280])
    kr = sb(esP, "kr_s", [128, 1280])
    rt1 = sb(esP, "rt1_s", [128, 640])
    rt2 = sb(esP, "rt2_s", [128, 640])
    krb = sb(esP, "krb_s", [128, 1280], BF16)
    v32 = sb(esP, "v32_s", [128, 512])
    ssv = sb(esP, "ssv_s", [128, 8])
    hT = sb(esP, "hTs", [128, 8, NTOK], BF16)
    ropeS = sb(esP, "ropeS", [NTOK, 64])
    ld(ropeS, ropeS.ap, rope_s)
    qk32 = sb(esP, "qk32", [NTOK, 1280])
    vv32 = sb(esP, "vv32", [NTOK, 768])
    u32 = sb(esP, "u32_s", [NTOK, 512])
    vn32 = sb(esP, "vn32_s", [NTOK, 512])
    vnb = sb(esP, "vnb_s", [NTOK, 512], BF16)
    iot = sb(esP, "iot", [128, 2])
    ld(iot, iot.ap, c_iota)
    pti = sb(esP, "pti", [128, NS * 16], I32)
    ld(pti, pti.ap, ptab[0:1, :].partition_broadcast(128))
    ptf = sb(esP, "ptf", [128, NS * 16])
    cp("dve", ptf, ptf.ap, pti, pti.ap)
    tsc("dve", kn, kn.ap[:, :NS * 16], ptf, ptf.ap, 128.0, iot.ap[:, 0:1], ALU.mult, ALU.add, extra=[iot])
    cp("dve", idx_tok, idx_tok.ap, kn, kn.ap[:, :NS * 16])
    pf3 = ptf.ap.rearrange("p (s q two) -> p s q two", s=NS, two=2)
    k3_ = kr.ap[:, :NS * 8].rearrange("p (s q) -> p s q", s=NS)
    tsc("dve", kr, k3_[0:64], ptf, pf3[0:64, :, :, 0], 64.0, iot.ap[0:64, 1:2], ALU.mult, ALU.add, extra=[iot])
    tsc("dve", kr, k3_[64:128], ptf, pf3[64:128, :, :, 1], 64.0, iot.ap[64:128, 1:2], ALU.mult, ALU.add, extra=[iot])
    cp("dve", idx_pr, idx_pr.ap, kr, kr.ap[:, :NS * 8])

    ld(xst, xst.ap, xs_d)
    norm_mod(xst, xst.ap, NTOK, mod[0], mod[1], hT, hT.ap)
    for (wt, c0, w_), do in zip([(wq, 0, 512), (wkv, 0, 512), (wkv, 512, 256)], [0, 512, 1024]):
        p = ps[0]
        for k in range(8):
            mm(p, p.ap[:NTOK, :w_], hT, hT.ap[:, k, :], wt, wt.ap[:, k, c0:c0 + w_], k == 0, k == 7)
        cp("act", qk32, qk32.ap[:, do:do + w_], p, p.ap[:NTOK, :w_])
    for c0, w_, do in ((768, 256, 0), (1024, 512, 256)):
        p = ps[1]
        for k in range(8):
            mm(p, p.ap[:NTOK, :w_], hT, hT.ap[:, k, :], wkv, wkv.ap[:, k, c0:c0 + w_], k == 0, k == 7)
        cp("act", vv32, vv32.ap[:, do:do + w_], p, p.ap[:NTOK, :w_])
    p = ps[0]
    for k in range(8):
        mm(p, p.ap[:NTOK, :], hT, hT.ap[:, k, :], wq, wq.ap[:, k, 512:1024], k == 0, k == 7)
    act(AF.Gelu_apprx_tanh, u32, u32.ap, p, p.ap[:NTOK, :])
    p = ps[1]
    for k in range(8):
        mm(p, p.ap[:NTOK, :], hT, hT.ap[:, k, :], wq, wq.ap[:, k, 1024:1536], k == 0, k == 7)
    vnorm_gelu(NTOK, p, vn32, vn32.ap)
    out_evs.append(S.dma("sp", lambda e: e.dma_start(out=schunk, in_=vn32.ap), reads=[vn32]))
    cp("act", vnb, vnb.ap, vn32, vn32.ap)
    p = ps[0]
    for k in range(8):
        mm(p, p.ap[:NTOK, 0:24], hT, hT.ap[:, k, :], wq, wq.ap[:, k, 1536:1560], k == 0, k == 7)
    act(AF.Sigmoid, gat, gat.ap, p, p.ap[:NTOK, 0:24])
    p = ps[1]
    for g in range(8):
        mm(p, p.ap[:NTOK, g * 64:(g + 1) * 64], ws4, ws4.ap[:NTOK, g, :NTOK], vnb, vnb.ap[:, g * 64:(g + 1) * 64], True, True)
    tt("dve", ALU.add, sqk, sqk.ap[:NTOK, 0:512], p, p.ap[:NTOK, :], bsbs, bsbs.ap[:NTOK])
    tt("pool", ALU.mult, mixb, mixb.ap[:, 0:512], sqk, sqk.ap[:NTOK, 0:512], u32, u32.ap)
    knorm_rope(qk32, qk32.ap, NTOK, 20, gainqk.ap[:NTOK, :], ropeS, ropeS.ap, kr, kr.ap[:NTOK, :])
    out_evs.append(S.dma("sp", lambda e: e.dma_start(out=sk[0], in_=kr.ap[:NTOK, 512:768]), reads=[kr]))
    out_evs.append(S.dma("sp", lambda e: e.dma_start(out=sk[2], in_=kr.ap[:NTOK, 768:1024]), reads=[kr]))
    out_evs.append(S.dma("sp", lambda e: e.dma_start(out=sk[1], in_=vv32.ap[:, 0:256]), reads=[vv32]))
    out_evs.append(S.dma("sp", lambda e: e.dma_start(out=sk[3], in_=vv32.ap[:, 256:512]), reads=[vv32]))
    out_evs.append(S.dma("sp", lambda e: e.dma_start(out=skw[:, 508:512, :], in_=kr.ap[:NTOK, 1024:1280]), reads=[kr]))
    out_evs.append(S.dma("sp", lambda e: e.dma_start(out=svw[:, 508:512, :], in_=vv32.ap[:, 512:768]), reads=[vv32]))
    out_evs.append(S.dma("sp", lambda e: e.dma_start(out=skw[:, 0:508, :], in_=kwin[:, 4:512, :])))
    out_evs.append(S.dma("sp", lambda e: e.dma_start(out=svw[:, 0:508, :], in_=vwin[:, 4:512, :])))
    cp("act", krb, krb.ap[:NTOK, 0:512].rearrange("p (q h d) -> p q h d", q=4, h=2), kr, kr.ap[:NTOK, 0:512].rearrange("p (h q d) -> p q h d", h=2, q=4))
    for kti, s0_ in enumerate((768, 1024)):
        cp(("act", "dve")[kti], krb, krb.ap[:NTOK, 512 + kti * 256:512 + (kti + 1) * 256].rearrange("p (s h d) -> p s h d", s=2, h=2),
           kr, kr.ap[:NTOK, s0_:s0_ + 256].rearrange("p (h s d) -> p s h d", h=2, s=2))
    for i4 in range(4):
        tr(pst.ap[:, i4 * 128:i4 * 128 + NTOK], krb, krb.ap[:NTOK, i4 * 128:(i4 + 1) * 128], NTOK)
    cp("act", QT, QT.ap, pst, pst.ap[:, 0:512].rearrange("p (s r t) -> p s r t", s=2, r=2)[:, :, :, :NTOK])
    for i4 in range(4):
        tr(pst.ap[:, 512 + i4 * 128:512 + i4 * 128 + NTOK], krb, krb.ap[:NTOK, 512 + i4 * 128:512 + (i4 + 1) * 128], NTOK)
    cp("act", KnT, KnT.ap[:, :, :, :NTOK], pst, pst.ap[:, 512:1024].rearrange("p (b s t) -> p b s t", b=2, s=2)[:, :, :, :NTOK])
    memset("pool", Vn, Vn.ap[:NTOK, :, :, 64:65], 1.0)
    cp("pool", Vn, Vn.ap[:NTOK, :, :, 0:64], vv32, vv32.ap[:, 256:768].rearrange("p (b g d) -> p b g d", b=2, g=4))
    close_scope(esP)
    chk("s0")

    NB = 127
    KT2s = [sb(esS, f"KT2s{i}", [128, 4104], BF16) for i in range(2)]
    for i in range(2):
        memset("pool", KT2s[i], KT2s[i].ap[:, 4096:4104], 0.0)
    rp = [sb(esS, f"rps{i}", [128, 512]) for i in range(3)]
    rpb = [sb(esS, f"rpbs{i}", [128, 512], BF16) for i in range(2)]
    hidT = sb(esS, "hidTs", [128, 2, 512], BF16)
    VccS = sb(esS, "VccS", [128, 4, 65], BF16)
    memset("pool", VccS, VccS.ap[:, :, 64:65], 1.0)
    PTs = [sb(esS, f"PTs{i}", [128, 2, NTOK], BF16) for i in range(4)]
    for i in range(4):
        memset("pool", PTs[i], PTs[i].ap, 0.0)
    STs = [T(ps[2].ap[:, 0:128], "STs0"), T(ps[3].ap[:, 0:128], "STs1")] * 2
    psE, psF, psG = ps[4], ps[5], ps[6]
    pi_ = [0]
    rpi = [0]
    c2 = [caches[i].rearrange("(m r) c -> m (r c)", r=2) for i in range(2)]

    def gather(dst_t, dst_ap, src_ap, idx_ap):
        S.dma("pool", lambda e: e.indirect_dma_start(out=dst_ap, out_offset=None, in_=src_ap,
                                                        in_offset=bass.IndirectOffsetOnAxis(ap=idx_ap, axis=0)),
              reads=[idx_tok, idx_pr], writes=[dst_t])


    def cmp_acc(g):
        return (psE, psF)[g // 2], (g % 2) * 196

    started = set()

    def first_on(bank):
        if bank.name in started:
            return False
        started.add(bank.name)
        return True

    for s in range(NS):
        for which in range(2):
            for q in range(8):
                a = rp[rpi[0] % 3]
                bq = rpb[rpi[0] % 2]
                rpi[0] += 1
                gather(a, a.ap, c2[which], idx_pr.ap[:, s * 8 + q:s * 8 + q + 1])
                cp(("act", "dve")[q % 2], bq, bq.ap.rearrange("p (g r d) -> p g r d", g=4, r=2), a, a.ap.rearrange("p (r g d) -> p g r d", r=2, g=4))
                for g in range(4):
                    tr(pst.ap[:, g * 128:(g + 1) * 128], bq, bq.ap[:, g * 128:(g + 1) * 128], 128)
                cp("act", KT2s[which], KT2s[which].ap[:, 0:4096].rearrange("p (g m) -> p g m", g=4)[:, :, q * 128:(q + 1) * 128], pst,
                   pst.ap[:, 0:512].rearrange("p (g t) -> p g t", g=4))
            w1, w2, b1 = cws[which]
            for hh in range(2):
                p = ps[hh]
                for q in range(16):
                    mm(p, p.ap, w1, w1.ap[:, q, hh * 128:(hh + 1) * 128], KT2s[which], KT2s[which].ap[:, q:q + 8 * 511 + 1:8], q == 0, q == 15)
                act(AF.Gelu_apprx_tanh, hidT, hidT.ap[:, hh, :], p, p.ap, extra=[b1], bias=b1.ap[:, hh:hh + 1])
            if which == 0:
                p = ps[0]
                for hh in range(2):
                    mm(p, p.ap, w2, w2.ap[:, hh, :], hidT, hidT.ap[:, hh, :], hh == 0, hh == 1)
                for g in range(4):
                    h0 = 64 * (g // 2)
                    cp("act", KccS, KccS.ap[h0:h0 + 64, g % 2, :NB], p, p.ap[h0:h0 + 64, g * 128:g * 128 + NB])
            else:
                p = ps[1]
                for g in range(4):
                    for hh in range(2):
                        mm(p, p.ap[:NB, g * 64:(g + 1) * 64], hidT, hidT.ap[:, hh, g * 128:g * 128 + NB], w2, w2.ap[:, hh, 0:64],
                           hh == 0, hh == 1)
                cp("act", VccS, VccS.ap[:NB, :, 0:64], p, p.ap[:NB, 0:256].rearrange("p (g d) -> p g d", d=64))
        for g in range(4):
            P0 = 64 * (g // 2)
            slot = g % 2
            stt_ = STs[pi_[0] % 4]
            ptb = PTs[pi_[0] % 4]
            pi_[0] += 1
            for r in range(2):
                mm(stt_, stt_.ap[:NB, r * 4:(r + 1) * 4], KccS, KccS.ap[P0:P0 + 64, slot, :NB],
                   QT, QT.ap[P0:P0 + 64, slot, r, s * 4:(s + 1) * 4], r == 0, True)
            act(AF.Exp, ptb, ptb.ap[:NB, :, s * 4:(s + 1) * 4], stt_, stt_.ap[:NB, 0:8].rearrange("p (r t) -> p r t", r=2), scale=0.125)
            bank, cb = cmp_acc(g)
            for r in range(2):
                c0 = cb + r * 98
                mm(bank, bank.ap[:NTOK, c0:c0 + 65], ptb, ptb.ap[:NB, r, :], VccS, VccS.ap[:NB, g, :], first_on(bank), False)
                mm(bank, bank.ap[:NTOK, c0 + 65:c0 + 98], ptb, ptb.ap[:NB, r, :], ovls, ovls.ap[:NB, :], False, False)
            memset("pool", ptb, ptb.ap[:NB, :, s * 4:(s + 1) * 4], 0.0)
        if s == 0:
            chk("s1")
    chk("s2")
    ec = sb(esS, "ec", [NTOK, 4, 2, 98])
    cp("act", ec, ec.ap[:, 0:2], psE, psE.ap[:NTOK, 0:392].rearrange("p (g r c) -> p g r c", g=2, r=2))
    cp("act", ec, ec.ap[:, 2:4], psF, psF.ap[:NTOK, 0:392].rearrange("p (g r c) -> p g r c", g=2, r=2))
    rcS = sb(esS, "rcS", [NTOK, 4, 2])
    impA = sb(esS, "impA", [NTOK, 4, NSBS])
    impB = sb(esS, "impB", [NTOK, 4, NSBS])
    scS = sb(esS, "scS", [NTOK, 4, NSBS])
    sc2 = sb(esS, "sc2S", [NTOK, NSBS])
    m8a = sb(esS, "m8aS", [NTOK, 8])
    m8b = sb(esS, "m8bS", [NTOK, 8])
    selbS = sb(esS, "selbS", [NTOK, 4, NSBS + 1], BF16)
    tsc("dve", rcS, rcS.ap, ec, ec.ap[:, :, :, 64], 1e-20, None, ALU.max)
    S.op("dve", lambda e: e.reciprocal(out=rcS.ap, in_=rcS.ap), reads=[rcS], writes=[rcS])
    tt("dve", ALU.mult, impA, impA.ap, ec, ec.ap[:, :, 0, 65:98], rcS, rcS.ap[:, :, 0:1].to_broadcast([NTOK, 4, NSBS]))
    tt("dve", ALU.mult, impB, impB.ap, ec, ec.ap[:, :, 1, 65:98], rcS, rcS.ap[:, :, 1:2].to_broadcast([NTOK, 4, NSBS]))
    tt("dve", ALU.add, impA, impA.ap, impA, impA.ap, impB, impB.ap)
    tt("dve", ALU.add, scS, scS.ap, impA, impA.ap, fs, fs.ap.unsqueeze(1).to_broadcast([NTOK, 4, NSBS]))
    memset("pool", selbS, selbS.ap, 0.0)
    for g in range(4):
        S.op("dve", lambda e: e.max(out=m8a.ap, in_=scS.ap[:, g, :]), reads=[scS], writes=[m8a])
        S.op("dve", lambda e: e.match_replace(out=sc2.ap, in_to_replace=m8a.ap, in_values=scS.ap[:, g, :], imm_value=-3e4),
             reads=[m8a, scS], writes=[sc2])
        S.op("dve", lambda e: e.max(out=m8b.ap, in_=sc2.ap), reads=[sc2], writes=[m8b])
        tsc("dve", sc2, sc2.ap, scS, scS.ap[:, g, :], m8b.ap[:, 7:8], -NEG, ALU.is_ge, ALU.mult, extra=[m8b])
        tsc("dve", selbS, selbS.ap[:, g, 0:NSBS], sc2, sc2.ap, NEG, None, ALU.add)

    selTS = sb(esS, "selTS", [32, 4, NTOK], BF16)
    e32s = sb(esS, "e32s", [32, 16, 128], BF16)
    ld(e32s, e32s.ap, c_e32[0:32])
    for g in range(4):
        tr(pst.ap[:32, g * 128:g * 128 + NTOK], selbS, selbS.ap[:, g, 0:32], NTOK)
    cp("act", selTS, selTS.ap, pst, pst.ap[:32, 0:512].rearrange("p (g t) -> p g t", g=4)[:, :, :NTOK])
    chk("s3")
    pgk = [sb(esS, f"pgk{i}", [128, 256]) for i in range(2)]
    pgv = [sb(esS, f"pgv{i}", [128, 256]) for i in range(2)]
    pgkb = [sb(esS, f"pgkb{i}", [128, 256], BF16) for i in range(2)]
    VsS = [sb(esS, f"VsS{i}", [128, 4, 65], BF16) for i in range(2)]
    for i in range(2):
        memset("pool", VsS[i], VsS[i].ap[:, :, 64:65], 1.0)
    wk32 = sb(esS, "wk32", [128, 4, 256])
    wv32 = sb(esS, "wv32", [128, 4, 256])
    wkb = sb(esS, "wkb", [128, 4, 256], BF16)
    VwS = sb(esS, "VwS", [128, 4, 4, 65], BF16)
    memset("pool", VwS, VwS.ap[:, :, :, 64:65], 1.0)
    selA = (psE, psF)
    winA = (psG, ps[1])
    started.clear()

    def attend_tile(s, g, kT_t, kT_ap, nkeys, bias_l, v_t, v_ap, accs):
        P0 = 64 * (g // 2)
        slot = g % 2
        stt_ = STs[pi_[0] % 4]
        ptb = PTs[pi_[0] % 4]
        pi_[0] += 1
        o8 = stt_.ap[:nkeys, 0:8].rearrange("p (r t) -> p r t", r=2)
        for r in range(2):
            mm(stt_, stt_.ap[:nkeys, r * 4:(r + 1) * 4], kT_t, kT_ap, QT, QT.ap[P0:P0 + 64, slot, r, s * 4:(s + 1) * 4], r == 0, bias_l is None)
        if bias_l is not None:
            b_t, b_ap, r_t, r_ap = bias_l
            for r in range(2):
                mm(stt_, stt_.ap[:nkeys, r * 4:(r + 1) * 4], b_t, b_ap, r_t, r_ap, False, r == 1)
        act(AF.Exp, ptb, ptb.ap[:nkeys, :, s * 4:(s + 1) * 4], stt_, o8, scale=0.125)
        bank = accs[g // 2]
        for r in range(2):
            c0 = (g % 2) * 130 + r * 65
            mm(bank, bank.ap[:NTOK, c0:c0 + 65], ptb, ptb.ap[:nkeys, r, :], v_t, v_ap, first_on(bank), False)
        memset("pool", ptb, ptb.ap[:nkeys, :, s * 4:(s + 1) * 4], 0.0)

    pgi = [0]
    for s in range(NS):
        for kt in range(16):
            i2 = pgi[0] % 2
            pgi[0] += 1
            gather(pgk[i2], pgk[i2].ap, caches[2], idx_tok.ap[:, s * 16 + kt:s * 16 + kt + 1])
            gather(pgv[i2], pgv[i2].ap, caches[3], idx_tok.ap[:, s * 16 + kt:s * 16 + kt + 1])
            cp("act", pgkb[i2], pgkb[i2].ap.rearrange("p (s h d) -> p s h d", s=2, h=2), pgk[i2], pgk[i2].ap.rearrange("p (h s d) -> p s h d", h=2, s=2))
            for slot in range(2):
                tr(pst.ap[:, slot * 128:(slot + 1) * 128], pgkb[i2], pgkb[i2].ap[:, slot * 128:(slot + 1) * 128], 128)
            cp("act", KsTs[i2], KsTs[i2].ap, pst, pst.ap[:, 0:256].rearrange("p (s t) -> p s t", s=2))
            cp("pool", VsS[i2], VsS[i2].ap[:, :, 0:64], pgv[i2], pgv[i2].ap.rearrange("p (g d) -> p g d", d=64))
            for g in range(4):
                P0 = 64 * (g // 2)
                attend_tile(s, g, KsTs[i2], KsTs[i2].ap[P0:P0 + 64, g % 2, :], 128,
                            (e32s, e32s.ap[0:32, kt, :], selTS, selTS.ap[0:32, g, s * 4:(s + 1) * 4]),
                            VsS[i2], VsS[i2].ap[:, g, :], selA)
        ld(wk32, wk32.ap, kwin[s].rearrange("(t p) c -> p t c", p=128))
        ld(wv32, wv32.ap, vwin[s].rearrange("(t p) c -> p t c", p=128))
        for wt in range(4):
            cp(("act", "dve")[wt % 2], wkb, wkb.ap[:, wt, :].rearrange("p (s h d) -> p s h d", s=2, h=2), wk32, wk32.ap[:, wt, :].rearrange("p (h s d) -> p s h d", h=2, s=2))
        cp("pool", VwS, VwS.ap[:, :, :, 0:64], wv32, wv32.ap.rearrange("p t (g d) -> p t g d", d=64))
        for wt in range(4):
            for slot in range(2):
                tr(pst.ap[:, (wt * 2 + slot) * 128:(wt * 2 + slot + 1) * 128], wkb, wkb.ap[:, wt, slot * 128:(slot + 1) * 128], 128)
        cp("act", KwTs, KwTs.ap, pst, pst.ap.rearrange("p (w s t) -> p w s t", w=4, s=2))
        for wt in range(4):
            for g in range(4):
                P0 = 64 * (g // 2)
                attend_tile(s, g, KwTs, KwTs.ap[P0:P0 + 64, wt, g % 2, :], 128,
                            (w0m, w0m.ap, ident, ident.ap[:NTOK, s * 4:(s + 1) * 4]) if wt == 0 else None, VwS, VwS.ap[:, wt, g, :], winA)
    chk("s5")
    for s in range(NS):
        for br, accs in ((0, selA), (1, winA)):
            for g in range(4):
                P0 = 64 * (g // 2)
                attend_tile(s, g, KnT, KnT.ap[P0:P0 + 64, br, g % 2, :], 128,
                            (bdm, bdm.ap, ident, ident.ap[:NTOK, s * 4:(s + 1) * 4]), Vn, Vn.ap[:, br, g, :], accs)
    chk("s6")
    rcs = sb(esS, "rcsS", [NTOK, 4, 2, 3])
    coef = sb(esS, "coefS", [NTOK, 4, 2, 3])
    bo32 = sb(esS, "bo32S", [NTOK, 512])
    cp("dve", rcs, rcs.ap[:, :, :, 0], rcS, rcS.ap)
    for gh in range(2):
        cp("dve", rcs, rcs.ap[:, 2 * gh:2 * gh + 2, :, 1], selA[gh], selA[gh].ap[:NTOK, 0:260].rearrange("p (g r c) -> p g r c", g=2, r=2)[:, :, :, 64])
        cp("dve", rcs, rcs.ap[:, 2 * gh:2 * gh + 2, :, 2], winA[gh], winA[gh].ap[:NTOK, 0:260].rearrange("p (g r c) -> p g r c", g=2, r=2)[:, :, :, 64])
    S.op("dve", lambda e: e.reciprocal(out=rcs.ap[:, :, :, 1:3], in_=rcs.ap[:, :, :, 1:3]), reads=[rcs], writes=[rcs])
    tt("dve", ALU.mult, coef, coef.ap, rcs, rcs.ap, gat, gat.ap.rearrange("p (g r k) -> p g r k", g=4, r=2))
    for g in range(4):
        for r in range(2):
            h = 2 * g + r
            dst = bo32.ap[:, h * 64:(h + 1) * 64]
            c0 = (g % 2) * 130 + r * 65
            tsc("dve", bo32, dst, ec, ec.ap[:, g, r, 0:64], coef.ap[:, g, r, 0:1], None, ALU.mult, extra=[coef])
            stt("dve", bo32, dst, selA[g // 2], selA[g // 2].ap[:NTOK, c0:c0 + 64], coef, coef.ap[:, g, r, 1:2], bo32, dst, ALU.mult, ALU.add)
            stt("dve", mixb, mixb.ap[:, 512 + h * 64:512 + (h + 1) * 64], winA[g // 2], winA[g // 2].ap[:NTOK, c0:c0 + 64],
                coef, coef.ap[:, g, r, 2:3], bo32, dst, ALU.mult, ALU.add)
    chk("s7")
    for k in range(8):
        tr(pst.ap[:, k * 128:k * 128 + NTOK], mixb, mixb.ap[:, k * 128:(k + 1) * 128], NTOK)
    cp("act", mixT, mixT.ap, pst, pst.ap.rearrange("p (k t) -> p k t", k=8)[:, :, :NTOK])
    x1s = sb(esS, "x1s", [NTOK, D])
    for hf in range(2):
        p = ps[2 + hf]
        for k in range(8):
            mm(p, p.ap[:NTOK, :], mixT, mixT.ap[:, k, :], wo, wo.ap[:, k, hf * 512:(hf + 1) * 512], k == 0, k == 7)
        tt("dve", ALU.mult, x1s, x1s.ap[:, hf * 512:(hf + 1) * 512], p, p.ap[:NTOK, :], mod[2], mod[2].ap[:NTOK, hf * 512:(hf + 1) * 512])
    tt("pool", ALU.add, x1s, x1s.ap, x1s, x1s.ap, xst, xst.ap)
    st(x1s_t.ap, x1s, x1s.ap, writes=[x1s_t])
    close_scope(esS)
    if cfg.stop == "smp":
        S.barrier()
        return nc

    esF = contextlib.ExitStack()
    alloc_scratch(esF)
    wf1 = sb(esF, "wf1", [128, 8, 5632], BF16)
    load_w(wf1, w_f1, 8, 5632)
    wf2 = sb(esF, "wf2", [128, 22, D], BF16)
    load_w(wf2, w_f2, 22, D)
    FB = 2
    hT2 = sb(esF, "hT2", [128, 8, FB * 128], BF16)
    actT = sb(esF, "actT", [128, 22, FB * 128], BF16)
    x1b = [sb(esF, f"x1b{i}", [128, D]) for i in range(FB)]
    sa = sb(esF, "sa", [128, FB * 128])
    yt = [sb(esF, f"yt{i}", [128, D]) for i in range(1)]
    yi = [0]

    def ffn_batch(tiles):
        offs = []
        o = 0
        for i, (xt_, npart, _) in enumerate(tiles):
            ld(x1b[i], x1b[i].ap[:npart], xt_.ap, reads=[xt_])
            norm_mod(x1b[i], x1b[i].ap[:npart], npart, mod[0], mod[1], hT2, hT2.ap[:, :, o:o + npart])
            offs.append(o)
            o += npart
        N = o
        for mch in range(22):
            pa, pb = ps[0 + (mch % 2) * 2], ps[1 + (mch % 2) * 2]
            for k in range(8):
                mm(pa, pa.ap[:, :N], wf1, wf1.ap[:, k, mch * 128:(mch + 1) * 128], hT2, hT2.ap[:, k, :N], k == 0, k == 7)
            for k in range(8):
                mm(pb, pb.ap[:, :N], wf1, wf1.ap[:, k, 2816 + mch * 128:2816 + (mch + 1) * 128], hT2, hT2.ap[:, k, :N], k == 0, k == 7)
            act(AF.Silu, sa, sa.ap[:, :N], pa, pa.ap[:, :N])
            tt("dve", ALU.mult, actT, actT.ap[:, mch, :N], sa, sa.ap[:, :N], pb, pb.ap[:, :N])
        for i, (xt_, npart, o_ap) in enumerate(tiles):
            y = yt[0]
            yi[0] += 1
            for hf in range(2):
                py = ps[4 + hf]
                for mch in range(22):
                    mm(py, py.ap[:npart, :], actT, actT.ap[:, mch, offs[i]:offs[i] + npart], wf2, wf2.ap[:, mch, hf * 512:(hf + 1) * 512],
                       mch == 0, mch == 21)
                tt("dve", ALU.mult, y, y.ap[:npart, hf * 512:(hf + 1) * 512], py, py.ap[:npart, :], mod[2], mod[2].ap[:npart, hf * 512:(hf + 1) * 512])
            tt("pool", ALU.add, y, y.ap[:npart], y, y.ap[:npart], x1b[i], x1b[i].ap[:npart])
            out_evs.append(S.dma("sp", lambda e: e.dma_start(out=o_ap, in_=y.ap[:npart]), reads=[y]))

    load_mod(1, False)
    for b0 in range(0, NOWN, FB):
        ffn_batch([(x1_t[m], 128, yown[m * 128:(m + 1) * 128, :]) for m in range(b0, min(NOWN, b0 + FB))])
    load_mod(1, True)
    ffn_batch([(x1s_t, NTOK, ys)])
    close_scope(esF)
    S.barrier()
    es0.close()
    print("instr counts", S.cnt, "waits", S.nwait, "dmas", {q: d["n"] for q, d in S.dq.items()})
    return nc


def _bf(a):
    return np.asarray(a, dtype=np.float32).astype(ml_dtypes.bfloat16)


def _rope_tab(pos):
    inv = (10000.0 ** (-np.arange(32, dtype=np.float32) / 32)).astype(np.float32)
    ang = pos.astype(np.float32)[:, None] * inv[None, :]
    return np.concatenate([np.cos(ang), np.sin(ang)], axis=1).astype(np.float32)


def make_in_maps(inp, cfg, cores):
    SEQ, NS, NPOOL = cfg.SEQ, cfg.NS, cfg.NPOOL
    NT = SEQ // 128
    NOWN = NT // 4
    NSB = SEQ // 64
    NCB = (SEQ - 32) // 16 + 1
    NCT = (NCB + 127) // 128
    NTOK = NS * 4
    NR = NS + 1
    NFY = NSB + 8 * (NOWN - 1)
    PAST = 2048
    f = lambda k: np.asarray(inp[k], dtype=np.float32)
    w_in = f("w_in")[0]
    cols = dict(u=(0, 512), v=(512, 1024), q=(1024, 1536), kc=(1536, 1792), vc=(1792, 2048), ks=(2048, 2304),
                vs=(2304, 2560), kw=(2560, 2816), vw=(2816, 3072), gl=(3072, 3096))
    cat = lambda names: np.ascontiguousarray(np.concatenate([w_in[:, cols[n][0]:cols[n][1]] for n in names], axis=1))
    w_kv = cat(["kc", "ks", "kw", "vc", "vs", "vw"])
    w_q = cat(["q", "u", "v", "gl"])
    w_s = f("w_sgu")[0]
    b_s = f("b_sgu")[0]
    w_sT = np.ascontiguousarray(w_s.transpose(2, 0, 1))
    w_s4 = np.ascontiguousarray(np.tile(w_s[:, 0:4, 0:4].transpose(2, 0, 1), (16, 1, 16)))
    b_sb = np.ascontiguousarray(np.repeat(b_s.T[:, :, None], 64, axis=2).reshape(128, 512))
    b_sbs = np.ascontiguousarray(np.tile(b_sb[0:4], (16, 1)))
    gains = np.concatenate([np.tile(f("g_q")[0], 8), np.tile(f("g_k_cmp")[0], 4), np.tile(f("g_k_slc")[0], 4),
                            np.tile(f("g_k_win")[0], 4)])[None, :].astype(np.float32)
    pe = lambda k: f(k)[0].reshape(16, 2, 64).transpose(1, 2, 0).reshape(128, 16)
    pe2 = np.ascontiguousarray(np.concatenate([pe("pe_k_cmp"), pe("pe_v_cmp")], axis=1))
    ii = np.arange(128)
    c_ident = _bf(np.eye(128))
    cidx = np.arange(NCT * 128)
    nn = np.arange(NSB)
    ovl = ((16 * cidx[:, None] < (nn[None, :] + 1) * 64) & (16 * cidx[:, None] + 32 > nn[None, :] * 64) & (cidx[:, None] < NCB))
    c_ovl = _bf(ovl.reshape(NCT, 128, NSB).transpose(1, 0, 2))
    c127 = np.arange(128)
    n33 = np.arange(33)
    c_ovls = _bf((16 * c127[:, None] < (n33[None, :] + 1) * 64) & (16 * c127[:, None] + 32 > n33[None, :] * 64) & (c127[:, None] < 127))
    c_fs = np.zeros((64, 33), np.float32)
    c_fs[:, [0, 31, 32]] = 1e4
    c_sel = np.zeros((NR, 192), np.float32)
    c_sel[NS, 0:128] = 1.0
    for s in range(NS):
        c_sel[s, 128 + 4 * s:128 + 4 * s + 4] = 1.0
    t64 = np.arange(64)
    same = (t64[:, None] // 4) == (t64[None, :] // 4)
    le = (t64[None, :] % 4) <= (t64[:, None] % 4)
    c_bd = _bf(np.where(same & le, 0.0, NEG))
    c_m4 = np.ascontiguousarray((same & le).T.astype(np.float32))
    c_w0 = _bf(np.where(ii[None, :] > (t64[:, None] % 4), 0.0, NEG))
    c_iota = np.stack([ii, ii % 64], axis=1).astype(np.float32)
    c_tril = (ii[:, None] <= ii[None, :]).astype(np.float32)
    kk16 = np.arange(16)
    kk32 = np.arange(32)
    c_e64 = _bf((ii[:, None, None] % 64) == (2 * kk32[None, :, None] + ii[None, None, :] // 64))
    c_e32 = _bf((ii[:, None, None] % 32) == (2 * kk16[None, :, None] + ii[None, None, :] // 64))
    rope_all = _rope_tab(np.arange(SEQ))
    rope_s = np.ascontiguousarray(np.tile(_rope_tab(PAST + np.arange(4)), (NS, 1)))
    shared = dict(w_ada=f("w_ada")[0], b_ada=f("b_ada"), g_mix=f("g_mix_norm"), g_ffn=f("g_ffn_norm"), w_kv=w_kv, w_q=w_q,
                  w_sT=w_sT, w_s4=w_s4, b_sb=b_sb, b_sbs=b_sbs, g_sgu=f("g_sgu"), gains=gains, pe2=pe2,
                  w_ck1=f("w_ck1")[0], w_cv1=f("w_cv1")[0], w_ck2=f("w_ck2")[0], w_cv2=f("w_cv2")[0], w_out=f("w_out")[0],
                  w_f1=f("w_ffn_in")[0], w_f2=f("w_ffn_out")[0], rope_all=rope_all, rope_s=rope_s, c_ident=c_ident,
                  c_ovl=c_ovl, c_ovls=c_ovls, c_fs=c_fs, c_sel=c_sel, c_bd=c_bd, c_m4=c_m4, c_w0=c_w0, c_iota=c_iota, c_tril=c_tril, c_e32=c_e32, c_e64=c_e64,
                  ckc=f("cache_k_cmp")[0].reshape(NPOOL * 128, 256), cvc=f("cache_v_cmp")[0].reshape(NPOOL * 128, 256),
                  cks=f("cache_k_slc")[0].reshape(NPOOL * 128, 256), cvs=f("cache_v_slc")[0].reshape(NPOOL * 128, 256))
    xp = f("x_prompt")
    xs = f("x_sample")
    maps = []
    for c in cores:
        b, j = c // 4, c % 4
        own = [4 * m + j for m in range(NOWN)]
        xbm = xp[b]
        xown = np.ascontiguousarray(np.concatenate([xbm[t * 128:(t + 1) * 128] for t in own], axis=0))
        pos_own = np.concatenate([np.arange(t * 128, (t + 1) * 128) for t in own])
        s0 = c * NS
        kp = np.arange(20)
        cl = np.arange(128)
        gj = np.where(16 * cl[None, None, :] + 31 <= 128 * (kp[None, :, None] - 3 + j) + ii[:, None, None], 0.0, NEG)
        y = np.arange(NFY)
        rel = y[None, :] - 8 * (NOWN - 1) - 2 * j
        cr = (ii[:, None] >= 64).astype(np.int64)
        fjt = np.where((rel == cr) | (rel == cr - 1), 1e4, np.where(rel > cr, -1e4, 0.0)).astype(np.float32)
        dl4 = np.arange(4)
        causal = np.where(cl[None, :] <= ii[:, None], 0.0, NEG)
        cm = np.where(dl4[None, :, None] < j, 0.0, np.where(dl4[None, :, None] == j, causal[:, None, :], NEG))
        dl8 = np.arange(8)
        off = j + 4 - dl8
        upper = np.where(cl[None, :] > ii[:, None], 0.0, NEG)
        wm = np.where(off[None, :, None] == 4, upper[:, None, :],
                      np.where((off[None, :, None] >= 1) & (off[None, :, None] <= 3), 0.0,
                               np.where(off[None, :, None] == 0, causal[:, None, :], NEG)))
        d = dict(shared)
        d.update(xb=xbm, xown=xown, xs=np.ascontiguousarray(xs[s0:s0 + NS].reshape(NTOK, D)),
                 cvec=np.ascontiguousarray(np.concatenate([f("c_sample")[s0:s0 + NS], f("c_prompt")[b:b + 1]], axis=0)),
                 kwin=np.ascontiguousarray(f("state_k_win")[0, s0:s0 + NS].reshape(NS, 512, 256)),
                 vwin=np.ascontiguousarray(f("state_v_win")[0, s0:s0 + NS].reshape(NS, 512, 256)),
                 ptab=np.ascontiguousarray(np.asarray(inp["page_table"])[s0:s0 + NS].reshape(1, NS * 16).astype(np.int32)),
                 rope_own=_rope_tab(pos_own), c_gjb=_bf(gj), c_fj=fjt, c_cm=_bf(cm), c_wm=_bf(wm))
        maps.append(d)
    return maps


def assemble(res, cfg, cores, nbatch, dec_batch):
    SEQ, NS = cfg.SEQ, cfg.NS
    NT = SEQ // 128
    NOWN = NT // 4
    f32 = np.float32
    yp = np.zeros((nbatch, SEQ, D), f32)
    ysm = np.zeros((dec_batch, 4, D), f32)
    pks = [np.zeros((1, nbatch, SEQ, 4, 64), f32) for _ in range(4)]
    pkw = np.zeros((1, nbatch, 512, 4, 64), f32)
    pvw = np.zeros((1, nbatch, 512, 4, 64), f32)
    pch = np.zeros((1, nbatch, 128, 512), f32)
    sks = [np.zeros((1, dec_batch, 4, 4, 64), f32) for _ in range(4)]
    skw = np.zeros((1, dec_batch, 512, 4, 64), f32)
    svw = np.zeros((1, dec_batch, 512, 4, 64), f32)
    sch = np.zeros((1, dec_batch, 4, 512), f32)
    for r, c in zip(res, cores):
        b, j = c // 4, c % 4
        for m in range(NOWN):
            t = 4 * m + j
            yp[b, t * 128:(t + 1) * 128] = r["yown"][m * 128:(m + 1) * 128]
        s0 = c * NS
        ysm[s0:s0 + NS] = r["ys"].reshape(NS, 4, D)
        if j == 0:
            for i, n in enumerate(("pkc", "pvc", "pks", "pvs")):
                pks[i][0, b] = r[n].reshape(SEQ, 4, 64)
            pkw[0, b] = r["pkw"].reshape(512, 4, 64)
            pvw[0, b] = r["pvw"].reshape(512, 4, 64)
        if j == 3:
            pch[0, b] = r["pchunk"]
        for i, n in enumerate(("skc", "svc", "sks", "svs")):
            sks[i][0, s0:s0 + NS] = r[n].reshape(NS, 4, 4, 64)
        skw[0, s0:s0 + NS] = r["skw"].reshape(NS, 512, 4, 64)
        svw[0, s0:s0 + NS] = r["svw"].reshape(NS, 512, 4, 64)
        sch[0, s0:s0 + NS] = r["schunk"].reshape(NS, 4, 512)
    return (yp, ysm, pks[0], pks[1], pks[2], pks[3], pkw, pvw, pch, sks[0], sks[1], sks[2], sks[3], skw, svw, sch)


def kernel(**inputs):
    cfg = Cfg()
    nc = build(cfg)
    cores = list(range(8))
    maps = make_in_maps(inputs, cfg, cores)
    res = run_bass_kernel_spmd(nc, maps, core_ids=cores)
    return assemble(res.results, cfg, cores, 2, 128)
```
